# Optimizing a Trainium2 kernel written in Bass

```python
import math
import jax, jax.numpy as jnp
from jax import lax
import numpy as np

D_MODEL = 1024
BATCH = 8
SEQ = 8192
DEPTH = 1
DEC_BATCH = 32
DEC_SEQ = 16
PAST_LEN = 2048

CHUNK = 64
Q_BLOCK = 128
DA_HEADS = 4
DA_HEAD_DIM = 64
DA_WIDTH = DA_HEADS * 2 * DA_HEAD_DIM
DN_HEADS = 4
DN_HEAD_DIM = 128
DN_WIDTH = DN_HEADS * DN_HEAD_DIM
CONV_WIDTH = 4
REL_BUCKETS = 32
REL_MAX_DIST = 128
IN_WIDTH = 4 * DA_WIDTH + 4 * DN_WIDTH + 2 * DN_HEADS + 2 * D_MODEL
DEEPNORM_ALPHA = (2.0 * DEPTH) ** 0.25
DEEPNORM_BETA = (8.0 * DEPTH) ** -0.25
LN_EPS = 1e-5
NORM_EPS = 1e-6

kernel_name = "diffattn_gdn_gated_merge_streaming_step"

F32 = jnp.float32


def _layernorm(x):
    xf = x.astype(F32)
    mu = jnp.mean(xf, axis=-1, keepdims=True)
    var = jnp.mean(jnp.square(xf - mu), axis=-1, keepdims=True)
    return (xf - mu) * lax.rsqrt(var + LN_EPS)


def _rmsnorm(x, w, eps):
    xf = x.astype(F32)
    return xf * lax.rsqrt(jnp.mean(xf * xf, axis=-1, keepdims=True) + eps) * w.astype(F32)


def _l2norm(x):
    return x * lax.rsqrt(jnp.sum(x * x, axis=-1, keepdims=True) + NORM_EPS)


def _rel_bucket(rel):
    nb = REL_BUCKETS // 2
    max_exact = nb // 2
    n = jnp.abs(rel)
    large = max_exact + (jnp.log(jnp.maximum(n, 1).astype(F32) / max_exact)
                         / math.log(REL_MAX_DIST / max_exact) * (nb - max_exact)).astype(jnp.int32)
    large = jnp.minimum(large, nb - 1)
    return jnp.where(rel > 0, nb, 0) + jnp.where(n < max_exact, n, large)


def _diff_attend(q, k, v, q_pos, k_pos, rel_table, lam):
    s = jnp.einsum('bqhcd,bkhcd->bchqk', q, k, preferred_element_type=F32) * (DA_HEAD_DIM ** -0.5)
    bias = rel_table.astype(F32)[_rel_bucket(k_pos[None, :] - q_pos[:, None])]
    s = s + jnp.transpose(bias, (2, 0, 1))[None, None]
    visible = (k_pos[None, :] // CHUNK) <= (q_pos[:, None] // CHUNK)
    p = jax.nn.softmax(jnp.where(visible, s, -jnp.inf), axis=-1)
    w = p[:, 0] - lam * p[:, 1]
    return jnp.einsum('bhqk,bkhe->bqhe', w.astype(v.dtype), v)


def _diff_attn_prompt(q, k, v, rel_table, lam):
    B, S = q.shape[:2]
    nb = S // Q_BLOCK
    qb = jnp.swapaxes(q.reshape((B, nb, Q_BLOCK) + q.shape[2:]), 0, 1)
    k_pos = jnp.arange(S, dtype=jnp.int32)

    def one(args):
        i, qi = args
        q_pos = i * Q_BLOCK + jnp.arange(Q_BLOCK, dtype=jnp.int32)
        return _diff_attend(qi, k, v, q_pos, k_pos, rel_table, lam)

    o = lax.map(one, (jnp.arange(nb, dtype=jnp.int32), qb))
    return jnp.swapaxes(o, 0, 1).reshape((B, S) + o.shape[3:])


def _diff_out(o, subln_w, lam_init, gate):
    B, L = o.shape[:2]
    o = (_rmsnorm(o, subln_w, LN_EPS) * (1.0 - lam_init)).reshape(B, L, DA_WIDTH)
    return o * jax.nn.silu(gate.astype(F32))


def _causal_conv(x, prev, w):
    xp = jnp.concatenate([prev, x], axis=1)
    L = x.shape[1]
    y = sum(xp[:, j:j + L] * w[j] for j in range(CONV_WIDTH))
    return jax.nn.silu(y), xp[:, xp.shape[1] - (CONV_WIDTH - 1):]


def _gdn_inputs(qkv, a, b, a_log, dt_bias):
    B, L, _ = qkv.shape
    heads = lambda t: t.reshape(B, L, DN_HEADS, DN_HEAD_DIM).transpose(0, 2, 1, 3)
    q, k, v = (heads(t) for t in jnp.split(qkv.astype(F32), 3, axis=-1))
    q = _l2norm(q) * (DN_HEAD_DIM ** -0.5)
    k = _l2norm(k)
    beta = jax.nn.sigmoid(b.astype(F32)).transpose(0, 2, 1)
    g = (-jnp.exp(a_log.astype(F32)) * jax.nn.softplus(a.astype(F32) + dt_bias.astype(F32))).transpose(0, 2, 1)
    return q, k, v, beta, g


def _gdn_prep(q, k, v, beta, g):
    C = g.shape[-1]
    G = jnp.cumsum(g, axis=-1)
    causal = jnp.tril(jnp.ones((C, C), dtype=bool))
    strict = jnp.tril(jnp.ones((C, C), dtype=bool), -1)
    decay = jnp.exp(jnp.where(causal, G[..., :, None] - G[..., None, :], -jnp.inf))
    kb = k * beta[..., None]
    lower = jnp.where(strict, jnp.einsum('...id,...jd->...ij', kb, k) * decay, 0.0)
    rhs = jnp.concatenate([v * beta[..., None], kb * jnp.exp(G)[..., None]], axis=-1)
    sol = lax.linalg.triangular_solve(lower, rhs, left_side=True, lower=True, unit_diagonal=True)
    u, w = sol[..., :DN_HEAD_DIM], sol[..., DN_HEAD_DIM:]
    qk = jnp.einsum('...id,...jd->...ij', q, k) * decay
    qg = q * jnp.exp(G)[..., None]
    kd = k * jnp.exp(G[..., -1:] - G)[..., None]
    gl = jnp.exp(G[..., -1])
    return u, w, qk, qg, kd, gl


def _gdn_step(s, u, w, qk, qg, kd, gl):
    v_new = u - jnp.einsum('...cd,...de->...ce', w, s)
    o = jnp.einsum('...cd,...de->...ce', qg, s) + jnp.einsum('...ij,...je->...ie', qk, v_new)
    s = s * gl[..., None, None] + jnp.einsum('...cd,...ce->...de', kd, v_new)
    return s, o


def _layer(x, c, p, rel_table, lam_init, cache):
    (w_ada, b_ada, w_in, lam_q1, lam_k1, lam_q2, lam_k2, subln_w, conv_w,
     a_log, dt_bias, dn_norm_w, w_pa, w_pb, w_out, ln_g, ln_b) = p
    dt = x.dtype
    B, L, _ = x.shape
    mod = jnp.dot(jax.nn.silu(c), w_ada) + b_ada
    shift, scale, gate = jnp.split(mod[:, None, :], 3, axis=-1)
    h = (_layernorm(x) * (1.0 + scale) + shift).astype(dt)
    sizes = (DA_WIDTH,) * 4 + (3 * DN_WIDTH, DN_WIDTH, DN_HEADS, DN_HEADS, D_MODEL)
    da_q, da_k, da_v, da_g, dn_qkv, dn_z, dn_a, dn_b, mg_a, mg_b = jnp.split(
        jnp.dot(h, w_in), np.cumsum(sizes).tolist(), axis=-1)

    lam = (jnp.exp(jnp.sum(lam_q1.astype(F32) * lam_k1.astype(F32)))
           - jnp.exp(jnp.sum(lam_q2.astype(F32) * lam_k2.astype(F32))) + lam_init)
    q = da_q.reshape(B, L, DA_HEADS, 2, DA_HEAD_DIM)
    k_rows = da_k.reshape(B, L, DA_HEADS, 2 * DA_HEAD_DIM)
    v_rows = da_v.reshape(B, L, DA_HEADS, 2 * DA_HEAD_DIM)
    if cache is None:
        o_a = _diff_attn_prompt(q, k_rows.reshape(B, L, DA_HEADS, 2, DA_HEAD_DIM), v_rows, rel_table, lam)
        conv_prev = jnp.zeros((B, CONV_WIDTH - 1, 3 * DN_WIDTH), dt)
        s0 = jnp.zeros((B, DN_HEADS, DN_HEAD_DIM, DN_HEAD_DIM), F32)
    else:
        k_past, v_past, conv_prev, s0 = cache
        P = k_past.shape[1]
        k_all = jnp.concatenate([k_past.astype(dt), k_rows], axis=1)
        v_all = jnp.concatenate([v_past.astype(dt), v_rows], axis=1)
        q_pos = P + jnp.arange(L, dtype=jnp.int32)
        k_pos = jnp.arange(P + L, dtype=jnp.int32)
        o_a = _diff_attend(q, k_all.reshape(B, P + L, DA_HEADS, 2, DA_HEAD_DIM), v_all,
                           q_pos, k_pos, rel_table, lam)
        s0 = s0.astype(F32)
    o_a = _diff_out(o_a, subln_w, lam_init, da_g).astype(dt)

    qkv, conv_tail = _causal_conv(dn_qkv, conv_prev.astype(dt), conv_w)
    q_b, k_b, v_b, beta, g = _gdn_inputs(qkv, dn_a, dn_b, a_log, dt_bias)
    if cache is None:
        n = L // CHUNK
        blk = lambda t: jnp.moveaxis(t.reshape(t.shape[:2] + (n, CHUNK) + t.shape[3:]), 2, 0)
        prep = _gdn_prep(*[blk(t) for t in (q_b, k_b, v_b, beta, g)])
        s_fin, o_b = lax.scan(lambda s, xs: _gdn_step(s, *xs), s0, prep)
        o_b = jnp.moveaxis(o_b, 0, 2).reshape(B, DN_HEADS, L, DN_HEAD_DIM)
    else:
        s_fin, o_b = _gdn_step(s0, *_gdn_prep(q_b, k_b, v_b, beta, g))
    o_b = o_b.transpose(0, 2, 1, 3)
    z = dn_z.reshape(B, L, DN_HEADS, DN_HEAD_DIM).astype(F32)
    o_b = (_rmsnorm(o_b, dn_norm_w, NORM_EPS) * jax.nn.silu(z)).reshape(B, L, DN_WIDTH).astype(dt)

    merged = jax.nn.sigmoid(mg_a) * jnp.dot(o_a, w_pa) + jax.nn.sigmoid(mg_b) * jnp.dot(o_b, w_pb)
    y = jnp.dot(merged, w_out)
    out = _layernorm(DEEPNORM_ALPHA * x + gate * y) * ln_g.astype(F32) + ln_b.astype(F32)
    return out.astype(dt), k_rows, v_rows, conv_tail, s_fin


def setup_inputs(seed: int = 0) -> dict:
    key = jax.random.key(seed)
    ks = jax.random.split(key, 32)
    nrm = lambda k, shape, s: jax.random.normal(k, shape, F32) * s
    col_scale = jnp.concatenate([
        jnp.ones((2 * DA_WIDTH,), F32), jnp.full((DA_WIDTH,), DEEPNORM_BETA, F32),
        jnp.ones((DA_WIDTH + 2 * DN_WIDTH,), F32), jnp.full((DN_WIDTH,), DEEPNORM_BETA, F32),
        jnp.ones((DN_WIDTH + 2 * DN_HEADS + 2 * D_MODEL,), F32)])
    dt0 = jnp.exp(jax.random.uniform(ks[20], (DEPTH, DN_HEADS), F32, math.log(1e-3), math.log(1e-1)))
    return {
        "x_prompt": nrm(ks[0], (BATCH, SEQ, D_MODEL), 1.0),
        "x_sample": nrm(ks[1], (DEC_BATCH, DEC_SEQ, D_MODEL), 1.0),
        "c_prompt": nrm(ks[2], (BATCH, D_MODEL), 1.0),
        "c_sample": nrm(ks[3], (DEC_BATCH, D_MODEL), 1.0),
        "cache_k": nrm(ks[4], (DEPTH, DEC_BATCH, PAST_LEN, DA_HEADS, 2 * DA_HEAD_DIM), 1.0),
        "cache_v": nrm(ks[5], (DEPTH, DEC_BATCH, PAST_LEN, DA_HEADS, 2 * DA_HEAD_DIM), 0.6),
        "state_conv": nrm(ks[6], (DEPTH, DEC_BATCH, CONV_WIDTH - 1, 3 * DN_WIDTH), 1.0),
        "state_delta": nrm(ks[7], (DEPTH, DEC_BATCH, DN_HEADS, DN_HEAD_DIM, DN_HEAD_DIM), 0.3),
        "w_ada": nrm(ks[8], (DEPTH, D_MODEL, 3 * D_MODEL), 0.5 * D_MODEL ** -0.5),
        "b_ada": nrm(ks[9], (DEPTH, 3 * D_MODEL), 0.01),
        "w_in": nrm(ks[10], (DEPTH, D_MODEL, IN_WIDTH), D_MODEL ** -0.5) * col_scale,
        "lam_q1": nrm(ks[11], (DEPTH, DA_HEAD_DIM), 0.1),
        "lam_k1": nrm(ks[12], (DEPTH, DA_HEAD_DIM), 0.1),
        "lam_q2": nrm(ks[13], (DEPTH, DA_HEAD_DIM), 0.1),
        "lam_k2": nrm(ks[14], (DEPTH, DA_HEAD_DIM), 0.1),
        "subln_w": 1.0 + nrm(ks[15], (DEPTH, 2 * DA_HEAD_DIM), 0.02),
        "conv_w": nrm(ks[16], (DEPTH, CONV_WIDTH, 3 * DN_WIDTH), CONV_WIDTH ** -0.5),
        "a_log": jnp.log(jax.random.uniform(ks[17], (DEPTH, DN_HEADS), F32, 1.0, 16.0)),
        "dt_bias": dt0 + jnp.log(-jnp.expm1(-dt0)),
        "dn_norm_w": 1.0 + nrm(ks[18], (DEPTH, DN_HEAD_DIM), 0.02),
        "w_pa": nrm(ks[19], (DEPTH, DA_WIDTH, D_MODEL), DA_WIDTH ** -0.5 * DEEPNORM_BETA),
        "w_pb": nrm(ks[21], (DEPTH, DN_WIDTH, D_MODEL), DN_WIDTH ** -0.5 * DEEPNORM_BETA),
        "w_out": nrm(ks[22], (DEPTH, D_MODEL, D_MODEL), D_MODEL ** -0.5 * DEEPNORM_BETA),
        "ln_g": 1.0 + nrm(ks[23], (DEPTH, D_MODEL), 0.02),
        "ln_b": nrm(ks[24], (DEPTH, D_MODEL), 0.02),
        "rel_table": nrm(ks[25], (REL_BUCKETS, DA_HEADS), 0.5),
    }


def reference(x_prompt, x_sample, c_prompt, c_sample, cache_k, cache_v, state_conv, state_delta,
              w_ada, b_ada, w_in, lam_q1, lam_k1, lam_q2, lam_k2, subln_w, conv_w, a_log, dt_bias,
              dn_norm_w, w_pa, w_pb, w_out, ln_g, ln_b, rel_table):
    y_prompt, y_sample = x_prompt, x_sample
    kp, vp, cp, sp, ksm, vsm, csm, ssm = [], [], [], [], [], [], [], []
    for l in range(DEPTH):
        lam_init = 0.8 - 0.6 * math.exp(-0.3 * l)
        p = (w_ada[l], b_ada[l], w_in[l], lam_q1[l], lam_k1[l], lam_q2[l], lam_k2[l], subln_w[l],
             conv_w[l], a_log[l], dt_bias[l], dn_norm_w[l], w_pa[l], w_pb[l], w_out[l], ln_g[l], ln_b[l])
        y_prompt, k_, v_, c_, s_ = _layer(y_prompt, c_prompt, p, rel_table, lam_init, None)
        kp.append(k_); vp.append(v_); cp.append(c_); sp.append(s_)
        y_sample, k_, v_, c_, s_ = _layer(y_sample, c_sample, p, rel_table, lam_init,
                                          (cache_k[l], cache_v[l], state_conv[l], state_delta[l]))
        ksm.append(k_); vsm.append(v_); csm.append(c_); ssm.append(s_)
    return (y_prompt, y_sample, jnp.stack(kp), jnp.stack(vp), jnp.stack(cp), jnp.stack(sp),
            jnp.stack(ksm), jnp.stack(vsm), jnp.stack(csm), jnp.stack(ssm))
```

```python
import math
from contextlib import ExitStack
import numpy as np
import concourse.bass as bass
import concourse.mybir as mybir
from concourse.bass_utils import run_bass_kernel_spmd

F32 = mybir.dt.float32
BF16 = mybir.dt.bfloat16
AF = mybir.ActivationFunctionType
ALU = mybir.AluOpType

S = 8192
D = 1024
NSM = 64
TT = S + NSM
NH = 4
PAST = 2048
QA, KA, VA, GA, DN, ZB, AB, MA, MB, WIN = 0, 512, 1024, 1536, 2048, 3584, 4096, 4104, 5128, 6152
NEG = -30000.0
ENGS = ("pe", "act", "dve", "pool", "sp")
C_ID, C_TRIU, C_MS, C_MDT, C_CM, C_ONE, C_SEL, C_OH, C_M16, C_MLO, C_END = 0, 128, 256, 384, 512, 640, 768, 960, 1344, 1472, 1856

PHASES = 99
CHECK = False
NAMES = {}
TRACE = False
DEBUG_SCR = False
LAST = None
SKIP = ()
EVE = None
DBG = 99


class Op:
    __slots__ = ("eng", "fn", "deps", "sig", "need", "dma", "tag", "idx", "line")


class Ctx:
    def __init__(self, nc):
        self.nc = nc
        self.sems = {e: nc.alloc_semaphore("s_" + e) for e in ENGS}
        self.cnt = {e: 0 for e in ENGS}
        self.tsems = {}
        self.tagcnt = {}
        self.tag_eng = {}


class Prog:
    def __init__(self, ctx, same_engine_sync=("act", "dve", "pool")):
        self.ctx = ctx
        self.nc = ctx.nc
        self.ops = []
        self.last_w = {}
        self.readers = {}
        self.same_sync = set(same_engine_sync)
        self.bank = {}

    def _add(self, eng, fn, reads, writes, dma=False, tag=None):
        if self.bank:
            extra = {self.bank[x] for x in list(reads) + list(writes) if x in self.bank}
            if extra:
                writes = list(writes) + [x for x in extra if x not in writes]
        o = Op()
        o.eng, o.fn, o.dma, o.tag, o.need, o.sig = eng, fn, dma, tag, False, None
        if TRACE:
            import sys as _sys
            f_ = _sys._getframe(1)
            while f_ is not None and f_.f_code.co_name in ("op", "dma", "_add", "MM", "TR", "ACT", "TS", "STT", "TTo", "CP", "RCP"):
                f_ = f_.f_back
            o.line = f_.f_lineno if f_ is not None else -1
        o.idx = len(self.ops)
        deps = set()
        for r in reads:
            w = self.last_w.get(r)
            if w is not None:
                deps.add(w)
        for w_ in writes:
            w = self.last_w.get(w_)
            if w is not None:
                deps.add(w)
            for rd in self.readers.get(w_, ()):
                deps.add(rd)
        o.deps = deps
        for r in reads:
            self.readers.setdefault(r, []).append(o.idx)
        for w_ in writes:
            self.last_w[w_] = o.idx
            self.readers[w_] = []
        self.ops.append(o)
        return o

    def op(self, eng, fn, reads=(), writes=()):
        return self._add(eng, fn, reads, writes)

    def dma(self, eng, out, in_, reads=(), writes=(), tag=None, **kw):
        eng = self.ctx.tag_eng.setdefault(tag, eng)
        return self._add(eng, lambda e: e.dma_start(out=out, in_=in_, **kw), reads, writes, dma=True, tag=tag)

    def _simulate(self, per, base_cnt, base_tag):
        ops = self.ops
        val = {("e", e): base_cnt[e] for e in ENGS}
        pc = {e: 0 for e in ENGS}
        progress = True
        while progress:
            progress = False
            for e in ENGS:
                while pc[e] < len(per[e]):
                    o = per[e][pc[e]]
                    ok = True
                    for d in o.deps:
                        y = ops[d]
                        if y.dma:
                            key = ("t", y.tag)
                        else:
                            if y.eng == e and not o.dma and e not in self.same_sync:
                                continue
                            key = ("e", y.eng)
                        if val.get(key, base_tag.get(key[1], 0) if key[0] == "t" else 0) < y.sig:
                            ok = False
                            break
                    if not ok:
                        break
                    if o.dma:
                        kk = ("t", o.tag)
                        val[kk] = val.get(kk, base_tag.get(o.tag, 0)) + 16
                        assert val[kk] == o.sig, (kk, val[kk], o.sig)
                    elif o.need:
                        val[("e", e)] += 1
                        assert val[("e", e)] == o.sig
                    pc[e] += 1
                    progress = True
        stuck = {e: (pc[e], len(per[e])) for e in ENGS if pc[e] < len(per[e])}
        print("SIM: ops", len(ops), "stuck", stuck, flush=True)
        assert not stuck

    def emit(self):
        nc, ctx, ops = self.nc, self.ctx, self.ops
        for o in ops:
            for d in o.deps:
                y = ops[d]
                if y.dma or y.eng != o.eng or o.dma or (y.eng in self.same_sync):
                    y.need = True
        base_cnt = dict(ctx.cnt)
        base_tag = dict(ctx.tagcnt)
        for o in ops:
            if o.dma:
                if o.tag not in ctx.tsems:
                    ctx.tsems[o.tag] = nc.alloc_semaphore("t%d" % len(ctx.tsems))
                    ctx.tagcnt[o.tag] = 0
                ctx.tagcnt[o.tag] += 16
                o.sig = ctx.tagcnt[o.tag]
            elif o.need:
                ctx.cnt[o.eng] += 1
                o.sig = ctx.cnt[o.eng]
        per = {e: [] for e in ENGS}
        for o in ops:
            per[o.eng].append(o)
        sems, tsems, tagcnt = ctx.sems, ctx.tsems, dict(ctx.tagcnt)
        if CHECK:
            self._simulate(per, base_cnt, base_tag)

        def run(engname):
            def body(e):
                seen = {}
                for o in per[engname]:
                    waits = {}
                    for d in o.deps:
                        y = ops[d]
                        if y.dma:
                            key = ("t", y.tag)
                        else:
                            if y.eng == engname and not o.dma and engname not in self.same_sync:
                                continue
                            key = ("e", y.eng)
                        if y.sig > waits.get(key, 0):
                            waits[key] = y.sig
                    for key, v in waits.items():
                        if seen.get(key, 0) >= v:
                            continue
                        seen[key] = v
                        e.wait_ge(tsems[key[1]] if key[0] == "t" else sems[key[1]], v)
                    ins = o.fn(e)
                    if TRACE:
                        NAMES[ins.ins.name] = o.line
                    if o.dma:
                        ins.then_inc(tsems[o.tag], 16)
                    elif o.need:
                        ins.then_inc(sems[o.eng], 1)
                done = set()
                for o in per[engname]:
                    if o.dma and o.tag not in done:
                        done.add(o.tag)
                        e.wait_ge(tsems[o.tag], tagcnt[o.tag])
            return body

        with nc.Block(no_gpsimd_drain=True) as block:
            block.tensor(run("pe"))
            block.scalar(run("act"))
            block.vector(run("dve"))
            block.gpsimd(run("pool"))
            block.sync(run("sp"))


def MM(P, out, lhsT, rhs, r, w, start=True, stop=True):
    P.op("pe", lambda e: e.matmul(out, lhsT, rhs, start=start, stop=stop), r, w)


def TR(P, out, in_, ident, r, w):
    P.op("pe", lambda e: e.transpose(out, in_, ident), r, w)


def ACT(P, out, in_, func, r, w, **kw):
    P.op("act", lambda e: e.activation(out=out, in_=in_, func=func, **kw), r, w)


def TS(P, eng, out, in0, s1, s2, op0, op1, r, w):
    if s2 is None:
        P.op(eng, lambda e: e.tensor_scalar(out, in0, s1, None, op0=op0), r, w)
    else:
        P.op(eng, lambda e: e.tensor_scalar(out, in0, s1, s2, op0=op0, op1=op1), r, w)


def STT(P, eng, out, in0, sc, in1, op0, op1, r, w):
    eng = "dve"
    P.op(eng, lambda e: e.scalar_tensor_tensor(out=out, in0=in0, scalar=sc, in1=in1, op0=op0, op1=op1), r, w)


def TTo(P, eng, out, in0, in1, op, r, w):
    P.op(eng, lambda e: e.tensor_tensor(out=out, in0=in0, in1=in1, op=op), r, w)


def CP(P, eng, out, in_, r, w):
    if eng == "act":
        P.op(eng, lambda e: e.copy(out, in_), r, w)
    else:
        P.op(eng, lambda e: e.tensor_copy(out, in_), r, w)


def RCP(P, out, in_, r, w):
    P.op("dve", lambda e: e.reciprocal(out, in_), r, w)


class Rot:
    def __init__(self, items):
        self.items = items
        self.i = 0

    def next(self):
        it = self.items[self.i % len(self.items)]
        self.i += 1
        return it


class K:
    pass


def _dq(i):
    return "sp" if i % 2 == 0 else "pool"


def phase0(k):
    nc = k.nc
    P = Prog(k.ctx)
    with ExitStack() as es:
        sb = lambda n, s, d=F32: es.enter_context(nc.sbuf_tensor(n, s, d))
        ps = lambda n, s, d=F32: es.enter_context(nc.psum_tensor(n, s, d))
        c5t = sb("c5t", [128, 8, 5])
        scT = sb("scT", [128, 8, 5])
        wad = [sb("wad%d" % i, [128, 8, 512]) for i in range(2)]
        bad = sb("bad", [128, 24])
        bg5 = sb("bg5", [5, 1024])
        g5 = sb("g5", [5, 1024])
        lamv = sb("lamv", [128, 4, 64])
        lt = sb("lt", [128, 2, 64])
        ls = sb("ls", [128, 2])
        tab = sb("tab", [32, 4])
        lth = sb("lth", [32, 128])
        Rsb = sb("Rsb", [128, 384])
        pf = sb("pf", [128, 2, 128])
        alog = sb("alog", [128, 4])
        psm = ps("psm", [128, 512])[:, 0:120]
        psg = [ps("psg%d" % i, [128, 512]) for i in range(2)]
        psb = ps("psb", [128, 512])
        for n_ in ("psm", "psg0", "psg1", "psb"):
            P.bank[n_] = n_

        cf = k.cf
        P.dma("sp", cf[:], k.d_consts, writes=["cf"], tag="cf")
        P.dma("pool", c5t[:], k.d_c5T, writes=["c5t"], tag="c5t")
        P.dma("pool", bad[:], k.d_badaT, writes=["bad"], tag="bad")
        P.dma("pool", bg5[:], k.d_bada[2048:3072].partition_broadcast(5), writes=["bg5"], tag="bg5")
        P.dma("pool", k.convw[:], k.d_convwT, writes=["convw"], tag="convw")
        for i, nm in enumerate(("lam_q1", "lam_k1", "lam_q2", "lam_k2")):
            P.dma("pool", lamv[:, i, :], k.d_small[nm].partition_broadcast(128), writes=["lamv%d" % i], tag="lamv%d" % i)
        P.dma("pool", k.wsub[:], k.d_small["subln_w"].partition_broadcast(128), writes=["wsub"], tag="wsub")
        P.dma("pool", k.dnw[:], k.d_small["dn_norm_w"].partition_broadcast(128), writes=["dnw"], tag="dnw")
        P.dma("pool", k.lng[:], k.d_small["ln_g"].partition_broadcast(128), writes=["lng"], tag="lng")
        P.dma("pool", k.lnb[:], k.d_small["ln_b"].partition_broadcast(128), writes=["lnb"], tag="lnb")
        P.dma("pool", k.dtb[:], k.d_small["dt_bias"].partition_broadcast(128), writes=["dtb"], tag="dtb")
        P.dma("pool", alog[:], k.d_small["a_log"].partition_broadcast(128), writes=["alog"], tag="alog")
        P.dma("pool", tab[:], k.d_rel, writes=["tab"], tag="tab")
        CP(P, "dve", k.ident_b[:], cf[:, C_ID:C_ID + 128], ["cf"], ["ident_b"])
        CP(P, "dve", k.ones_b[:], cf[:, C_ONE:C_ONE + 128], ["cf"], ["ones_b"])
        P.op("pool", lambda e: e.memset(k.eps5[:], 1e-5), (), ["eps5"])
        P.op("pool", lambda e: e.memset(k.eps6[:], 1e-6), (), ["eps6"])
        TS(P, "dve", k.wsub[:], k.wsub[:], 0.8, None, ALU.mult, None, ["wsub"], ["wsub"])
        P.dma("pool", k.wsubc[:], k.d_small["subln_w"].rearrange("(p o) -> p o", o=1), writes=["wsubc"], tag="wsubc")
        TS(P, "dve", k.wsubc[:], k.wsubc[:], 0.8, None, ALU.mult, None, ["wsubc"], ["wsubc"])
        ACT(P, k.negA[:], alog[:], AF.Exp, ["alog"], ["negA"])
        TS(P, "dve", k.negA[:], k.negA[:], -1.0, None, ALU.mult, None, ["negA"], ["negA"])
        TTo(P, "dve", lt[:, 0, :], lamv[:, 0, :], lamv[:, 1, :], ALU.mult, ["lamv0", "lamv1"], ["lt0"])
        TTo(P, "dve", lt[:, 1, :], lamv[:, 2, :], lamv[:, 3, :], ALU.mult, ["lamv2", "lamv3"], ["lt1"])
        P.op("dve", lambda e: e.reduce_sum(ls[:], lt[:], axis=mybir.AxisListType.X), ["lt0", "lt1"], ["ls"])
        ACT(P, ls[:], ls[:], AF.Exp, ["ls"], ["ls"])
        TTo(P, "dve", k.nlam[:], ls[:, 1:2], ls[:, 0:1], ALU.subtract, ["ls"], ["nlam"])
        TS(P, "dve", k.nlam[:], k.nlam[:], -0.2, None, ALU.add, None, ["nlam"], ["nlam"])
        ACT(P, scT[:], c5t[:], AF.Silu, ["c5t"], ["scT"])
        wv = k.d_wada.rearrange("(kc p) n -> p kc n", p=128)
        for g in range(6):
            w = wad[g % 2]
            wk = "wad%d" % (g % 2)
            P.dma(_dq(g), w[:], wv[:, :, g * 512:(g + 1) * 512], writes=[wk], tag=wk)
            for fc in range(4):
                col = (g * 4 + fc) * 5
                for kc in range(8):
                    MM(P, psm[:, col:col + 5], w[:, kc, fc * 128:(fc + 1) * 128], scT[:, kc, :], [wk, "scT"], ["psm"],
                       start=(kc == 0), stop=(kc == 7))
            if g >= 4:
                for kc in range(8):
                    MM(P, psg[g - 4][0:5, :], scT[:, kc, :], w[:, kc, :], [wk, "scT"], ["psg%d" % (g - 4)],
                       start=(kc == 0), stop=(kc == 7))
        for fc in range(24):
            TS(P, "dve", k.modT[:, fc, :], psm[:, fc * 5:fc * 5 + 5], bad[:, fc:fc + 1],
               1.0 if 8 <= fc < 16 else 0.0, ALU.add, ALU.add, ["psm", "bad"], ["modT"])
        for hf in range(2):
            TTo(P, "dve", g5[:, hf * 512:(hf + 1) * 512], psg[hf][0:5, :], bg5[:, hf * 512:(hf + 1) * 512], ALU.add,
                ["psg%d" % hf, "bg5"], ["g5"])
        for hf in range(2):
            MM(P, psb[:, :], cf[0:5, C_SEL:C_SEL + 128], g5[:, hf * 512:(hf + 1) * 512], ["cf", "g5"], ["psb"])
            CP(P, "dve", k.gate_p[:, hf * 512:(hf + 1) * 512], psb[:, :], ["psb"], ["gate_p"])
            MM(P, psb[0:64, :], cf[0:5, C_SEL + 128:C_SEL + 192], g5[:, hf * 512:(hf + 1) * 512], ["cf", "g5"], ["psb"])
            CP(P, "dve", k.gate_s[:, hf * 512:(hf + 1) * 512], psb[0:64, :], ["psb"], ["gate_s"])
        for h in range(NH):
            TS(P, "dve", lth[:], cf[0:32, C_ONE:C_ONE + 128], tab[:, h:h + 1], None, ALU.mult, None, ["cf", "tab"], ["lth"])
            MM(P, psb[:, 0:384], lth[:], cf[0:32, C_OH:C_OH + 384], ["lth", "cf"], ["psb"])
            CP(P, "dve", Rsb[:], psb[:, 0:384], ["psb"], ["Rsb"])
            P.dma("sp", k.d_R[h], Rsb[:], reads=["Rsb"], writes=["dR"], tag="dR")
            base = h * 128 * 384
            P.dma("sp", pf[:, 0, :], bass.AP(k.d_R_t, base + 127, [[383, 128], [1, 128]]), reads=["dR"], writes=["pf"], tag="pf")
            P.dma("sp", pf[:, 1, :], bass.AP(k.d_R_t, base + 255, [[383, 128], [1, 128]]), reads=["dR"], writes=["pf"], tag="pf")
            TTo(P, "dve", k.pat[:, h, 0, :], pf[:, 0, :], cf[:, C_CM:C_CM + 128], ALU.add, ["pf", "cf"], ["pat"])
            CP(P, "dve", k.pat[:, h, 1, :], pf[:, 1, :], ["pf"], ["pat"])
        P.emit()


def phase1(k):
    nc = k.nc
    with ExitStack() as es0:
        winb = es0.enter_context(nc.sbuf_tensor("winb", [128, 8, WIN], BF16))
        P = Prog(k.ctx)
        with ExitStack() as es:
            wst = [es.enter_context(nc.sbuf_tensor("wst%d" % i, [128, 8, 512], F32)) for i in range(2)]
            wv = k.d_win.rearrange("(kc p) n -> p kc n", p=128)
            for g in range(13):
                c0 = g * 512
                cw = min(512, WIN - c0)
                w = wst[g % 2]
                wk = "wst%d" % (g % 2)
                P.dma(_dq(g), w[:, :, 0:cw], wv[:, :, c0:c0 + cw], writes=[wk], tag=wk)
                for kc in range(8):
                    CP(P, "dve" if kc % 2 == 0 else "act", winb[:, kc, c0:c0 + cw], w[:, kc, 0:cw], [wk], ["winb"])
            P.emit()
        P = Prog(k.ctx)
        with ExitStack() as es:
            sb = lambda n, s, d=F32: es.enter_context(nc.sbuf_tensor(n, s, d))
            ps = lambda n, s, d=F32: es.enter_context(nc.psum_tensor(n, s, d))
            xt = [sb("xt%d" % i, [128, 1024]) for i in range(2)]
            st = sb("st", [128, 2, 6])
            mv = sb("mv", [128, 2])
            rstd = sb("rstd", [128, 1])
            xn = sb("xn", [128, 4, 1024], BF16)
            hTs = [sb("hT%d" % i, [128, 8, 512], BF16) for i in range(2)]
            dnT = sb("dnT", [128, 12, 515])
            dnS = sb("dnS", [128, 12, 4, 19])
            acc = sb("acc", [128, 512])
            yv = sb("yv", [128, 512])
            yb = [sb("yb%d" % i, [128, 512], BF16) for i in range(2)]
            sq = sb("sq", [128, 512], BF16)
            rn = sb("rn", [128, 512])
            fm = Rot([(sb("fm%d" % i, [128, 512], BF16), "fm%d" % i) for i in range(4)])
            tf = Rot([(sb("tf%d" % i, [128, 512]), "tf%d" % i) for i in range(3)])
            tb = Rot([(sb("tb%d" % i, [128, 512], BF16), "tb%d" % i) for i in range(4)])
            gb = sb("gb", [128, 4, 8])
            t4 = sb("t4", [128, 4])
            psA = Rot([(ps("psA%d" % i, [128, 512]), "psA%d" % i) for i in range(3)])
            psB = psA
            psn = ps("psn", [128, 512])
            psT = [ps("psT%d" % i, [128, 2, 4, 128], BF16) for i in range(2)]
            psX = Rot([(ps("psX%d" % i, [128, 8, 128], BF16)[:, 0:4, :], "psX%d" % i) for i in range(1)])
            ps8 = ps("ps8", [128, 512])[:, 0:8]
            for n_ in ("psA0", "psA1", "psA2", "psn", "psT0", "psT1", "psX0", "ps8"):
                P.bank[n_] = n_
            ident = k.ident_b
            P.op("pool", lambda e: e.memset(dnT[:, :, 0:3], 0.0), (), ["dnT"])
            for s in range(4):
                for r_ in range(3):
                    P.dma("sp", dnS[:, :, s, r_], k.d_sconv[s, r_].rearrange("(i p) -> p i", p=128), writes=["dnT"], tag="dnSin",
                          allow_slow_non_contiguous=True)
            tiles = [(i * 512, 4, 128, False) for i in range(S // 512)] + [(S, 1, 64, True)]
            dq = 0

            def tile_gen(ti, t0, nsub, R, smp):
                nonlocal dq
                hT = hTs[ti % 2]
                hk = "hT%d" % (ti % 2)
                T = nsub * R
                for j in range(nsub):
                    xs = xt[j % 2]
                    xk = "xt%d" % (j % 2)
                    src = k.d_xs if smp else k.d_xp[t0 + j * R:t0 + (j + 1) * R, :]
                    P.dma(_dq(j), xs[0:R, :], src, writes=[xk], tag=xk)
                    P.op("dve", lambda e, a=st[0:R, 0, :], b_=xs[0:R, 0:512]: e.bn_stats(a, b_), [xk], ["st0"])
                    P.op("dve", lambda e, a=st[0:R, 1, :], b_=xs[0:R, 512:1024]: e.bn_stats(a, b_), [xk], ["st1"])
                    P.op("dve", lambda e, a=mv[0:R, :], b_=st[0:R, :, :]: e.bn_aggr(a, b_), ["st0", "st1"], ["mv"])
                    ACT(P, rstd[0:R, :], mv[0:R, 1:2], AF.Sqrt, ["mv", "eps5"], ["rstd"], bias=k.eps5[0:R, :], scale=1.0)
                    RCP(P, rstd[0:R, :], rstd[0:R, :], ["rstd"], ["rstd"])
                    TS(P, "dve", xn[0:R, j, :], xs[0:R, :], mv[0:R, 0:1], rstd[0:R, :], ALU.subtract, ALU.mult,
                       [xk, "mv", "rstd"], ["xn%d" % j])
                yield
                for fc in range(8):
                    px, pk = psX.next()
                    for j in range(nsub):
                        TR(P, px[:, j, 0:R], xn[0:R, j, fc * 128:(fc + 1) * 128], ident[0:R, 0:R], ["xn%d" % j, "ident_b"], [pk])
                    if not smp:
                        ACT(P, hT[:, fc, :], px.rearrange("p a b -> p (a b)"), AF.Identity, [pk, "modT"], [hk],
                            scale=k.modT[:, 8 + fc, 0:1], bias=k.modT[:, fc, 0:1])
                    else:
                        for s in range(4):
                            ACT(P, hT[:, fc, s * 16:(s + 1) * 16], px[:, 0, s * 16:(s + 1) * 16], AF.Identity, [pk, "modT"], [hk],
                                scale=k.modT[:, 8 + fc, 1 + s:2 + s], bias=k.modT[:, fc, 1 + s:2 + s])
                if False:
                    pass
                yield
                fmlist = ([("q", QA, i) for i in range(4)] + [("k", KA, i) for i in range(4)] + ([] if smp else [("g", GA, i) for i in range(4)])
                          + [("dn", DN, i) for i in range(12)]
                          + [("ma", MA, i) for i in range(8)] + [("mb", MB, i) for i in range(8)])
                for n_, (kind, cb, i) in enumerate(fmlist):
                    if n_ == len(fmlist) // 2:
                        yield
                    pa, pk = psA.next()
                    col = cb + i * 128
                    for kc in range(8):
                        MM(P, pa[:, 0:T], winb[:, kc, col:col + 128], hT[:, kc, 0:T], ["winb", hk], [pk], start=(kc == 0), stop=(kc == 7))
                    if kind == "dn":
                        if not smp:
                            CP(P, "dve" if i % 2 else "act", dnT[:, i, 3:3 + T], pa[:, 0:T], [pk], ["dnT"])
                        else:
                            CP(P, "dve", dnS[:, i, :, 3:19], pa[:, 0:64].rearrange("p (s c) -> p s c", c=16), [pk], ["dnT"])
                        continue
                    f, fk = fm.next()
                    if kind == "q":
                        ACT(P, f[:, 0:T], pa[:, 0:T], AF.Copy, [pk], [fk], scale=0.125)
                        dst = k.d_qT
                    elif kind == "k":
                        CP(P, "dve", f[:, 0:T], pa[:, 0:T], [pk], [fk])
                        dst = k.d_kT
                    elif kind == "g":
                        ACT(P, f[:, 0:T], pa[:, 0:T], AF.Silu, [pk], [fk])
                        dst = k.d_sgT
                    else:
                        ACT(P, f[:, 0:T], pa[:, 0:T], AF.Sigmoid, [pk], [fk])
                        dst = k.d_smA if kind == "ma" else k.d_smB
                    dq += 1
                    P.dma(_dq(dq), dst[i * 128:(i + 1) * 128, t0:t0 + T], f[:, 0:T], reads=[fk], writes=[fk], tag=fk)
                yield
                units = []

                def tok_unit(j, kind, cb, R=R, t0=t0, smp=smp):
                    nonlocal dq
                    tok0 = t0 + j * R
                    hsl = slice(j * R, (j + 1) * R)
                    if True:
                        pb, pk = psB.next()
                        for kc in range(8):
                            MM(P, pb[0:R, :], hT[:, kc, hsl], winb[:, kc, cb:cb + 512], ["winb", hk], [pk], start=(kc == 0), stop=(kc == 7))
                        if kind in ("k", "v"):
                            f, fk = tf.next()
                            CP(P, "dve" if kind == "k" else "act", f[0:R, :], pb[0:R, :], [pk], [fk])
                            if smp:
                                dst = (k.o_nks if kind == "k" else k.o_nvs)[:, :]
                            else:
                                dst = (k.o_nkp if kind == "k" else k.o_nvp)[tok0:tok0 + R, :]
                            dq += 1
                            if kind == "v":
                                b, bk = tb.next()
                                CP(P, "dve", b[0:R, :], f[0:R, :], [fk], [bk])
                                P.dma(_dq(dq + 1), k.d_vb[tok0:tok0 + R, :], b[0:R, :], reads=[bk], writes=[bk], tag=bk)
                            P.dma(_dq(dq), dst, f[0:R, :], reads=[fk], writes=[fk], tag=fk)
                        else:
                            b, bk = tb.next()
                            ACT(P, b[0:R, :], pb[0:R, :], AF.Silu, [pk], [bk])
                            dq += 1
                            P.dma(_dq(dq), (k.d_sg if kind == "g" else k.d_sz)[tok0:tok0 + R, :], b[0:R, :], reads=[bk], writes=[bk], tag=bk)

                def ab_unit(j, R=R, t0=t0):
                    tok0 = t0 + j * R
                    hsl = slice(j * R, (j + 1) * R)
                    for kc in range(8):
                        MM(P, ps8[0:R, :], hT[:, kc, hsl], winb[:, kc, AB:AB + 8], ["winb", hk], ["ps8"], start=(kc == 0), stop=(kc == 7))
                    TTo(P, "dve", t4[0:R, :], ps8[0:R, 0:4], k.dtb[0:R, :], ALU.add, ["ps8", "dtb"], ["t4"])
                    ACT(P, t4[0:R, :], t4[0:R, :], AF.Exp, ["t4"], ["t4"])
                    ACT(P, t4[0:R, :], t4[0:R, :], AF.Ln, ["t4"], ["t4"], bias=1.0, scale=1.0)
                    TTo(P, "dve", gb[0:R, j, 0:4], t4[0:R, :], k.negA[0:R, :], ALU.mult, ["t4", "negA"], ["gb"])
                    ACT(P, gb[0:R, j, 4:8], ps8[0:R, 4:8], AF.Sigmoid, ["ps8"], ["gb"])
                    P.dma("sp", k.d_gbeta[tok0:tok0 + R, :], gb[0:R, j, :], reads=["gb"], writes=["gb"], tag="gb")

                for j in range(nsub):
                    for kind, cb in ((("k", KA), ("v", VA), ("g", GA), ("z", ZB)) if smp else (("k", KA), ("v", VA), ("z", ZB))):
                        units.append(lambda j=j, kind=kind, cb=cb: tok_unit(j, kind, cb))
                    units.append(lambda j=j: ab_unit(j))
                for i in range(12):
                    cw = k.convw
                    if not smp:
                        xv = lambda o: dnT[:, i, o:o + T]
                        av = acc[:, 0:T]
                    else:
                        xv = lambda o: dnS[:, i, :, o:o + 16]
                        av = acc[:, 0:64].rearrange("p (s c) -> p s c", c=16)
                    TS(P, "dve", av, xv(0), cw[:, i, 0:1], None, ALU.mult, None, ["dnT", "convw"], ["acc"])
                    for jj in (1, 2):
                        STT(P, "pool", av, xv(jj), cw[:, i, jj:jj + 1], av, ALU.mult, ALU.add, ["dnT", "convw", "acc"], ["acc"])
                    STT(P, "dve", av, xv(3), cw[:, i, 3:4], av, ALU.mult, ALU.add, ["dnT", "convw", "acc"], ["acc"])
                    if units:
                        units.pop(0)()
                    if i < 8:
                        ACT(P, yv[:, 0:T], acc[:, 0:T], AF.Silu, ["acc"], ["yv"])
                        ACT(P, sq[:, 0:T], yv[:, 0:T], AF.Square, ["yv"], ["sq"])
                        MM(P, psn[:, 0:T], k.ones_b[:], sq[:, 0:T], ["ones_b", "sq"], ["psn"])
                        ACT(P, rn[:, 0:T], psn[:, 0:T], AF.Sqrt, ["psn", "eps6"], ["rn"], bias=k.eps6[:, :], scale=1.0)
                        RCP(P, rn[:, 0:T], rn[:, 0:T], ["rn"], ["rn"])
                        y, yk = yb[i % 2], "yb%d" % (i % 2)
                        STT(P, "dve", y[:, 0:T], yv[:, 0:T], (128.0 ** -0.5) if i < 4 else 1.0, rn[:, 0:T], ALU.mult, ALU.mult,
                            ["yv", "rn"], [yk])
                        dq += 1
                        if DBG >= 3.4:
                            P.dma(_dq(dq), (k.d_qbT if i < 4 else k.d_kbT)[(i % 4) * 128:(i % 4 + 1) * 128, t0:t0 + T], y[:, 0:T],
                                  reads=[yk], writes=[yk], tag=yk)
                    else:
                        y, yk = yb[i % 2], "yb%d" % (i % 2)
                        ACT(P, y[:, 0:T], acc[:, 0:T], AF.Silu, ["acc"], [yk])
                    if units:
                        units.pop(0)()
                    if i >= 4 and DBG >= 3.6:
                        for j in range(nsub):
                            TR(P, psT[j // 2][0:R, j % 2, i % 4, :], y[:, j * R:(j + 1) * R], ident[:, :], [yk, "ident_b"], ["psT%d" % (j // 2)])
                        if i % 4 == 3 and DBG >= 3.7:
                            for j in range(nsub):
                                b, bk = tb.next()
                                CP(P, EVE if EVE else ("dve" if j % 2 else "act"), b[0:R, :], psT[j // 2][0:R, j % 2, :, :].rearrange("p a b -> p (a b)"),
                                   ["psT%d" % (j // 2)], [bk])
                                dq += 1
                                tok0 = t0 + j * R
                                if DBG >= 3.8:
                                  P.dma(_dq(dq), (k.d_kb if i == 7 else k.d_vbn)[tok0:tok0 + R, :], b[0:R, :], reads=[bk], writes=[bk], tag=bk)
                if False:
                    pass
                while units:
                    units.pop(0)()
                if not smp:
                    if t0 + T == S:
                        for r_ in range(3):
                            P.dma("sp", k.o_ncp[r_].rearrange("(i p) -> p i", p=128), dnT[:, :, T + r_], reads=["dnT"], tag="nco",
                                  allow_slow_non_contiguous=True)
                    else:
                        CP(P, "pool", dnT[:, :, 0:3], dnT[:, :, T:T + 3], ["dnT"], ["dnT"])
                else:
                    for s in range(4):
                        for r_ in range(3):
                            P.dma("sp", k.o_ncs[s, r_].rearrange("(i p) -> p i", p=128), dnS[:, :, s, 16 + r_], reads=["dnT"], tag="nco",
                                  allow_slow_non_contiguous=True)

            gens = [tile_gen(ti, *tl) for ti, tl in enumerate(tiles)]

            def adv(g):
                try:
                    next(g)
                except StopIteration:
                    pass
            adv(gens[0])
            adv(gens[0])
            for i_ in range(len(gens)):
                adv(gens[i_])
                if i_ + 1 < len(gens):
                    adv(gens[i_ + 1])
                adv(gens[i_])
                if i_ + 1 < len(gens):
                    adv(gens[i_ + 1])
                adv(gens[i_])
            P.emit()


def attn_post(P, k, h, R, O1s, O2s, sgt, sgk, oaT_dst, wk, pst, tag):
    nc = k.nc
    for qs, ((o1, k1), (o2, k2)) in enumerate(zip(O1s, O2s)):
        r1, r2, o1s, od, ss, junk, oab, oat = wk
        RCP(P, r1[0:R, :], o1[:, 128:129], [k1], ["ap_r1"])
        TS(P, "dve", o1s[0:R, :], o1[:, 0:128], r1[0:R, :], None, ALU.mult, None, [k1, "ap_r1"], ["ap_o1s"])
        RCP(P, r2[0:R, :], o2[:, 128:129], [k2], ["ap_r2"])
        TTo(P, "dve", r2[0:R, :], r2[0:R, :], k.nlam[0:R, :], ALU.mult, ["ap_r2", "nlam"], ["ap_r2"])
        STT(P, "dve", od[0:R, :], o2[:, 0:128], r2[0:R, :], o1s[0:R, :], ALU.mult, ALU.add, [k2, "ap_r2", "ap_o1s"], ["ap_od"])
        ACT(P, junk[0:R, :], od[0:R, :], AF.Square, ["ap_od"], ["ap_junk", "ap_ss"], accum_out=ss[0:R, :])
        ACT(P, ss[0:R, :], ss[0:R, :], AF.Sqrt, ["ap_ss", "eps5"], ["ap_ss"], bias=k.eps5[0:R, :], scale=1.0 / 128.0)
        RCP(P, ss[0:R, :], ss[0:R, :], ["ap_ss"], ["ap_ss"])
        STT(P, "dve", od[0:R, :], od[0:R, :], ss[0:R, :], k.wsub[0:R, :], ALU.mult, ALU.mult, ["ap_od", "ap_ss", "wsub"], ["ap_od"])
        TTo(P, "dve", oab[0:R, :], od[0:R, :], sgt(qs), ALU.mult, ["ap_od", sgk], ["ap_oab"])
        TR(P, pst[:, 0:R], oab[0:R, :], k.ident_b[0:R, 0:R], ["ap_oab", "ident_b"], ["pst"])
        CP(P, "dve", oat[:, qs * R:(qs + 1) * R], pst[:, 0:R], ["pst"], [tag])


def phase2(k):
    nc = k.nc
    P = Prog(k.ctx)
    with ExitStack() as es:
        sb = lambda n, s, d=F32: es.enter_context(nc.sbuf_tensor(n, s, d))
        ps = lambda n, s, d=F32: es.enter_context(nc.psum_tensor(n, s, d))
        KT = [sb("KT%d" % i, [128, S], BF16) for i in range(2)]
        QT = [sb("QT%d" % i, [128, S], BF16) for i in range(2)]
        V = [sb("V%d" % i, [128, S // 128, 128], BF16) for i in range(2)]
        pT = Rot([(sb("pT%d" % i, [128, 512], BF16), "pT%d" % i) for i in range(8)])
        sgt = [sb("sgt%d" % i, [128, 512], BF16) for i in range(2)]
        oat = [sb("oat%d" % i, [128, 512], BF16) for i in range(2)]
        o1s = [sb("o1s%d" % i, [128, 512]) for i in range(2)]
        o2s = [sb("o2s%d" % i, [128, 512]) for i in range(2)]
        r1 = sb("a_r1", [128, 512])
        r2 = sb("a_r2", [128, 512])
        od = sb("a_od", [128, 512])
        sq = sb("a_sq", [128, 512], BF16)
        rs = sb("a_rs", [128, 512])
        psS = Rot([(ps("psS%d" % i, [128, 512]), "psS%d" % i) for i in range(4)])
        psOT = [ps("psOT%d" % c, [128, 512]) for c in range(2)]
        psRS = [ps("psRS%d" % c, [128, 512]) for c in range(2)]
        for n_ in ("psS0", "psS1", "psS2", "psS3", "psOT0", "psOT1", "psRS0", "psRS1"):
            P.bank[n_] = n_
        pending = []
        qbi = 0
        for h in range(NH):
            b = h % 2
            P.dma("sp", KT[b][:], k.d_kT[h * 128:(h + 1) * 128, 0:S], writes=["KT%d" % b], tag="KT%d" % b)
            P.dma("pool", QT[b][:], k.d_qT[h * 128:(h + 1) * 128, 0:S], writes=["QT%d" % b], tag="QT%d" % b)
            P.dma("sp", V[b][:], k.d_vb[0:S, h * 128:(h + 1) * 128].rearrange("(n p) e -> p n e", p=128),
                  writes=["V%d" % b], tag="V%d" % b)
            kt, qt, v = KT[b], QT[b], V[b]
            rk = ["KT%d" % b, "QT%d" % b]
            for Qb in range(S // 512):
                sb_i = qbi % 2
                qbi += 1
                P.dma("pool", sgt[sb_i][:], k.d_sgT[h * 128:(h + 1) * 128, Qb * 512:(Qb + 1) * 512], writes=["sgt%d" % sb_i], tag="sgt%d" % sb_i)
                nkb = 4 * Qb + 4

                def qk(kb, c):
                    cs = slice(c * 64, (c + 1) * 64)
                    j = kb - 4 * Qb
                    col0 = max(j, 0) * 128
                    pS, pk = psS.next()
                    needb = j >= -1
                    MM(P, pS[:, col0:512], kt[cs, kb * 128:(kb + 1) * 128], qt[cs, Qb * 512 + col0:(Qb + 1) * 512], rk, [pk],
                       start=True, stop=not needb)
                    if needb:
                        lst = [(qs, qs - j) for qs in range(4) if (qs - j) in (0, 1)]
                        for n_, (qs, d) in enumerate(lst):
                            MM(P, pS[:, qs * 128:(qs + 1) * 128], k.ident_b[:], k.pat[:, h, d, :], ["ident_b", "pat"], [pk],
                               start=False, stop=(n_ == len(lst) - 1))
                    p_, ptk = pT.next()
                    ACT(P, p_[:, col0:512], pS[:, col0:512], AF.Exp, [pk], [ptk])
                    return (kb, col0, p_, ptk)

                def pv(infos):
                    kb, col0 = infos[0][0], infos[0][1]
                    last = (kb == nkb - 1)
                    for c in range(2):
                        MM(P, psOT[c][:, col0:512], v[:, kb, :], infos[c][2][:, col0:512], ["V%d" % b, infos[c][3]], ["psOT%d" % c],
                           start=(kb == 0), stop=last)
                    for c in range(2):
                        MM(P, psRS[c][:, col0:512], k.ones_b[:], infos[c][2][:, col0:512], ["ones_b", infos[c][3]], ["psRS%d" % c],
                           start=(kb == 0), stop=last)

                fifo = []
                for kb in range(nkb):
                    fifo.append((qk(kb, 0), qk(kb, 1)))
                    if len(fifo) > 1:
                        pv(fifo.pop(0))
                    if kb == 2 and pending:
                        pending.pop()()
                while fifo:
                    pv(fifo.pop(0))
                o1, o2 = o1s[sb_i], o2s[sb_i]
                RCP(P, r1[:], psRS[0][:], ["psRS0"], ["a_r1"])
                TTo(P, "dve", o1[:], psOT[0][:], r1[:], ALU.mult, ["psOT0", "a_r1"], ["o1s%d" % sb_i])
                RCP(P, r2[:], psRS[1][:], ["psRS1"], ["a_r2"])
                TTo(P, "dve", o2[:], psOT[1][:], r2[:], ALU.mult, ["psOT1", "a_r2"], ["o2s%d" % sb_i])

                def post(h=h, Qb=Qb, sb_i=sb_i, o1=o1, o2=o2):
                    ot, otk = oat[sb_i], "oat%d" % sb_i
                    STT(P, "dve", od[:], o2[:], k.nlam[:, :], o1[:], ALU.mult, ALU.add, ["o2s%d" % sb_i, "o1s%d" % sb_i, "nlam"], ["a_od"])
                    ACT(P, sq[:], od[:], AF.Square, ["a_od"], ["a_sq"])
                    pL, plk = psS.next()
                    MM(P, pL[:], k.ones_b[:], sq[:], ["ones_b", "a_sq"], [plk])
                    ACT(P, rs[:], pL[:], AF.Sqrt, [plk, "eps5"], ["a_rs"], bias=k.eps5[:, :], scale=1.0 / 128.0)
                    RCP(P, rs[:], rs[:], ["a_rs"], ["a_rs"])
                    TTo(P, "dve", od[:], od[:], rs[:], ALU.mult, ["a_od", "a_rs"], ["a_od"])
                    STT(P, "dve", ot[:], od[:], k.wsubc[:, :], sgt[sb_i][:], ALU.mult, ALU.mult, ["a_od", "wsubc", "sgt%d" % sb_i], [otk])
                    P.dma("sp", k.d_oaT[h * 128:(h + 1) * 128, Qb * 512:(Qb + 1) * 512], ot[:], reads=[otk], writes=[otk], tag=otk)
                if pending:
                    pending.pop()()
                pending.append(post)
        while pending:
            pending.pop()()
        P.emit()


def phase2s(k):
    nc = k.nc
    NBK = PAST // 128
    P = Prog(k.ctx)
    with ExitStack() as es:
        sb = lambda n, s, d=F32: es.enter_context(nc.sbuf_tensor(n, s, d))
        ps = lambda n, s, d=F32: es.enter_context(nc.psum_tensor(n, s, d))
        kfs = [sb("kf%d" % i, [128, NBK, 512]) for i in range(2)]
        kbf = sb("kbf", [128, NBK, 512], BF16)
        KTs = sb("KTs", [128, NBK, 4, 128], BF16)
        vfs = [sb("vf0", [128, NBK, 512])] * 2
        Vs = sb("Vs", [128, NBK, 4, 129], BF16)
        Vn = sb("Vn", [16, 4, 129], BF16)
        Qblk = sb("Qblk", [128, 4, 32], BF16)
        KTn = sb("KTn", [128, 4, 16], BF16)
        pTs = sb("pTs", [128, NBK * 32], BF16)
        pTn = sb("pTn", [16, 32], BF16)
        sgs = sb("sgs", [16, 512], BF16)
        oat = sb("oats", [128, 16], BF16)
        wk = (sb("s_r1", [128, 1]), sb("s_r2", [128, 1]), sb("s_o1s", [128, 128]), sb("s_od", [128, 128]), sb("s_ss", [128, 1]),
              sb("s_junk", [128, 128]), sb("s_oab", [128, 128], BF16), oat)
        psS = ps("psS", [128, 512])[:, 0:NBK * 32]
        psN = ps("psN", [128, 512])[0:16, 0:32]
        psO = [ps("psOs%d" % c, [128, 512])[0:16, 0:129] for c in range(2)]
        psK = Rot([(ps("psK%d" % i, [128, 8, 128], BF16)[:, 0:4, :], "psK%d" % i) for i in range(2)])
        pst = ps("pst2", [128, 1024], BF16)[:, 0:128]
        for n_ in ("psS", "psN", "psOs0", "psOs1", "psK0", "psK1", "pst"):
            P.bank[n_] = n_
        P.op("pool", lambda e: e.memset(Qblk[:], 0.0), (), ["Qblk"])
        P.op("pool", lambda e: e.memset(Vs[:, :, :, 128:129], 1.0), (), ["Vs"])
        P.op("pool", lambda e: e.memset(Vn[:, :, 128:129], 1.0), (), ["Vn"])
        def ldcache(s_):
            P.dma("sp", kfs[s_ % 2][:], k.d_ck[s_].rearrange("(n p) e -> p n e", p=128), writes=["kf%d" % (s_ % 2)], tag="kf%d" % (s_ % 2))

        def ldv(s_):
            P.dma("pool", vfs[0][:], k.d_cv[s_].rearrange("(n p) e -> p n e", p=128), writes=["vf0"], tag="vf0")
        ldcache(0)
        ldv(0)
        for s in range(4):
            tk0 = S + s * 16
            if s + 1 < 4:
                ldcache(s + 1)
            kf, vf = kfs[s % 2], vfs[s % 2]
            kfk, vfk = "kf%d" % (s % 2), "vf0"
            qv = k.d_qT[:, tk0:tk0 + 16].rearrange("(h p) t -> p h t", p=128)
            P.dma("sp", Qblk[0:64, :, 0:16], qv[0:64], writes=["Qblk"], tag="Qblk")
            P.dma("sp", Qblk[64:128, :, 16:32], qv[64:128], writes=["Qblk"], tag="Qblk")
            P.dma("pool", KTn[:, :, :], k.d_kT[:, tk0:tk0 + 16].rearrange("(h p) t -> p h t", p=128), writes=["KTn"], tag="KTn")
            P.dma("pool", Vn[:, :, 0:128], k.d_vb[tk0:tk0 + 16, :].rearrange("p (h e) -> p h e", e=128), writes=["Vn"], tag="Vn")
            P.dma("pool", sgs[:], k.d_sg[tk0:tk0 + 16, :], writes=["sgs"], tag="sgs")
            for n in range(NBK):
                CP(P, "dve" if n % 2 else "act", kbf[:, n, :], kf[:, n, :], [kfk], ["kbf"])
                CP(P, "act" if n % 2 else "dve", Vs[:, n, :, 0:128], vf[:, n, :].rearrange("p (h e) -> p h e", e=128), [vfk], ["Vs"])
            if s + 1 < 4:
                ldv(s + 1)
            for n in range(NBK):
                pk_, pkk = psK.next()
                for hh in range(4):
                    TR(P, pk_[:, hh, :], kbf[:, n, hh * 128:(hh + 1) * 128], k.ident_b[:], ["kbf", "ident_b"], [pkk])
                CP(P, "act" if n % 2 else "dve", KTs[:, n, :, :], pk_, [pkk], ["KTs"])
            for h in range(NH):
                for n in range(NBK):
                    MM(P, psS[:, n * 32:(n + 1) * 32], KTs[:, n, h, :], Qblk[:, h, :], ["KTs", "Qblk"], ["psS"], start=True, stop=(n != NBK - 1))
                for c in range(2):
                    MM(P, psS[:, (NBK - 1) * 32 + c * 16:(NBK - 1) * 32 + (c + 1) * 16], k.ident_b[:], k.pat[:, h, 1, 0:16], ["ident_b", "pat"], ["psS"],
                       start=False, stop=(c == 1))
                MM(P, psN[:, :], KTn[:, h, :], Qblk[:, h, :], ["KTn", "Qblk"], ["psN"], start=True, stop=False)
                for c in range(2):
                    MM(P, psN[:, c * 16:(c + 1) * 16], k.ident_b[0:16, 0:16], k.pat[0:16, h, 0, 0:16], ["ident_b", "pat"], ["psN"],
                       start=False, stop=(c == 1))
                ACT(P, pTs[:], psS[:], AF.Exp, ["psS"], ["pTs"])
                ACT(P, pTn[:], psN[:], AF.Exp, ["psN"], ["pTn"])
                for c in range(2):
                    for n in range(NBK):
                        MM(P, psO[c][:, :], pTs[:, n * 32 + c * 16:n * 32 + (c + 1) * 16], Vs[:, n, h, :], ["pTs", "Vs"], ["psOs%d" % c],
                           start=(n == 0), stop=False)
                    MM(P, psO[c][:, :], pTn[:, c * 16:(c + 1) * 16], Vn[:, h, :], ["pTn", "Vn"], ["psOs%d" % c], start=False, stop=True)
                attn_post(P, k, h, 16, [(psO[0][:, :], "psOs0")], [(psO[1][:, :], "psOs1")],
                          lambda qs, h=h: sgs[:, h * 128:(h + 1) * 128], "sgs", None, wk, pst, "oats")
                P.dma("sp", k.d_oaT[h * 128:(h + 1) * 128, tk0:tk0 + 16], oat[:], reads=["oats"], writes=["oats"], tag="oats")
        P.emit()


class Rec:
    def __init__(self):
        self.items = []

    def op(self, eng, fn, reads=(), writes=()):
        self.items.append((eng, fn, reads, writes))


def phase3(k):
    nc = k.nc
    P = Prog(k.ctx)
    with ExitStack() as es:
        sb = lambda n, s, d=F32: es.enter_context(nc.sbuf_tensor(n, s, d))
        ps = lambda n, s, d=F32: es.enter_context(nc.psum_tensor(n, s, d))
        cf = k.cf
        NB = 2
        qT = [sb("g_qT%d" % i, [128, 4, 128], BF16) for i in range(NB)]
        kTt = [sb("g_kT%d" % i, [128, 4, 128], BF16) for i in range(NB)]
        kb = [sb("g_kb%d" % i, [128, 512], BF16) for i in range(NB)]
        vb = [sb("g_vb%d" % i, [128, 512], BF16) for i in range(NB)]
        szb = [sb("g_sz%d" % i, [128, 512], BF16) for i in range(NB)]
        gbt = [sb("g_gb%d" % i, [128, 8]) for i in range(NB)]
        st = [sb("g_s%d" % h, [128, 128]) for h in range(NH)]
        stb = [sb("g_sb%d" % h, [128, 128], BF16) for h in range(NH)]
        Gcol = sb("Gcol", [128, 4])
        nGcol = sb("nGcol", [128, 4])
        obt = sb("obt", [128, 512], BF16)
        obT = [sb("obT%d" % i, [128, 4, 128], BF16) for i in range(2)]
        f32n = ("gones", "Grow", "tmpa", "tmpb", "Dm", "DTm", "eGrow", "Lf", "Tnf", "TTf", "usb", "junk", "ob")
        bfn = ("Lo", "W1T", "Tnb", "Pk0", "Pk1", "PkT0", "PkT1", "TTb", "vbeta", "kbg", "kd", "qgT", "qkT", "wT", "vnew")
        coln = ("gl", "kds", "eGc", "bge", "ss")
        H = []
        for h in range(NH):
            d = {}
            for n_ in f32n:
                d[n_] = sb("h%d_%s" % (h, n_), [128, 128])
            for n_ in bfn:
                d[n_] = sb("h%d_%s" % (h, n_), [128, 128], BF16)
            for n_ in coln:
                d[n_] = sb("h%d_%s" % (h, n_), [128, 1])
            H.append(d)
        HB = [ps("gHB%d" % h, [128, 512]) for h in range(NH)]
        SB = ps("gSB", [128, 8, 128], BF16)
        GC = ps("gGC", [128, 512])
        p_ot = SB[:, 0:4, :]
        p_gc = GC[:, 0:4]
        P.bank["p_ot"] = "gSB"
        P.bank["p_gc"] = "gGC"
        PS = []
        for h in range(NH):
            A_, B_, C_ = HB[h][:, 0:128], HB[h][:, 128:256], HB[h][:, 256:384]
            d = dict(p_g=A_, p_a=A_, p_u=A_, p_ws=A_, p_kk=B_, p_b=B_, p_w=B_, p_o=B_, p_c=C_, p_qk=C_, p_s=C_, p_tr=SB[:, 4 + h, :])
            PS.append(d)
            for n_, sl_ in (("p_g", "A"), ("p_a", "A"), ("p_u", "A"), ("p_ws", "A"), ("p_kk", "B"), ("p_b", "B"), ("p_w", "B"), ("p_o", "B"),
                            ("p_c", "C"), ("p_qk", "C"), ("p_s", "C")):
                P.bank["pslot%s_%d" % (sl_, h)] = "gHB%d" % h
            P.bank["pslotT_%d" % h] = "gSB"
        SLOT = dict(p_g="A", p_a="A", p_u="A", p_ws="A", p_kk="B", p_b="B", p_w="B", p_o="B", p_c="C", p_qk="C", p_s="C", p_tr="T")
        ident = k.ident_b

        def head_ops(R, C, bi, h, nsteps):
            kq, kk_, kkb, kvb, ksz, kgb = ["g_%s%d" % (n, bi) for n in ("qT", "kT", "kb", "vb", "sz", "gb")]
            g8 = gbt[bi]
            T_ = H[h]
            pp = PS[h]
            K_ = lambda n: "h%d_%s" % (h, n)
            pk_ = lambda n: "pslot%s_%d" % (SLOT[n], h)
            hs = slice(h * 128, (h + 1) * 128)
            gones, Grow, tmpa, tmpb, Dm, DTm, eGrow, Lf, Tnf, TTf, usb, junk, ob = [T_[n] for n in f32n]
            Lo, W1T, Tnb, Pk0, Pk1, PkT0, PkT1, TTb, vbeta, kbg, kd, qgT, qkT, wT, vnew = [T_[n] for n in bfn]
            gl, kds, eGc, bge, ss = [T_[n] for n in coln]
            Pk, PkT = [Pk0, Pk1], [PkT0, PkT1]
            p_g, p_kk, p_qk, p_u, p_a, p_b, p_c, p_w, p_ws, p_o, p_s, p_tr = [pp[n] for n in
                ("p_g", "p_kk", "p_qk", "p_u", "p_a", "p_b", "p_c", "p_w", "p_ws", "p_o", "p_s", "p_tr")]
            TS(R, "dve", gones[0:C, :], cf[0:C, C_ONE:C_ONE + 128], g8[0:C, h:h + 1], None, ALU.mult, None, ["cf", kgb], [K_("gones")])
            MM(R, p_g[:, 0:C], gones[0:C, :], cf[0:C, C_TRIU:C_TRIU + C], [K_("gones"), "cf"], [pk_("p_g")])
            CP(R, "act", Grow[:, 0:C], p_g[:, 0:C], [pk_("p_g")], [K_("Grow")])
            TTo(R, "dve", tmpa[0:C, 0:C], cf[0:C, C_MS:C_MS + C], Grow[0:C, 0:C], ALU.subtract, ["cf", K_("Grow")], [K_("tmpa")])
            ACT(R, Dm[0:C, 0:C], tmpa[0:C, 0:C], AF.Exp, [K_("tmpa"), "Gcol"], [K_("Dm")], bias=Gcol[0:C, h:h + 1], scale=1.0)
            TTo(R, "dve", tmpb[0:C, 0:C], cf[0:C, C_MDT:C_MDT + C], Grow[0:C, 0:C], ALU.add, ["cf", K_("Grow")], [K_("tmpb")])
            ACT(R, DTm[0:C, 0:C], tmpb[0:C, 0:C], AF.Exp, [K_("tmpb"), "nGcol"], [K_("DTm")], bias=nGcol[0:C, h:h + 1], scale=1.0)
            ACT(R, eGrow[:, 0:C], Grow[:, 0:C], AF.Exp, [K_("Grow")], [K_("eGrow")])
            ACT(R, gl[:, :], Grow[:, C - 1:C], AF.Exp, [K_("Grow")], [K_("gl")])
            ACT(R, kds[0:C, :], Gcol[0:C, h:h + 1], AF.Exp, ["Gcol", K_("Grow")], [K_("kds")], bias=Grow[0:C, C - 1:C], scale=-1.0)
            ACT(R, eGc[0:C, :], Gcol[0:C, h:h + 1], AF.Exp, ["Gcol"], [K_("eGc")])
            TTo(R, "dve", bge[0:C, :], g8[0:C, 4 + h:5 + h], eGc[0:C, :], ALU.mult, [kgb, K_("eGc")], [K_("bge")])
            MM(R, p_kk[0:C, 0:C], kTt[bi][:, h, 0:C], kTt[bi][:, h, 0:C], [kk_], [pk_("p_kk")])
            STT(R, "dve", Lf[0:C, 0:C], p_kk[0:C, 0:C], g8[0:C, 4 + h:5 + h], Dm[0:C, 0:C], ALU.mult, ALU.mult,
                [pk_("p_kk"), kgb, K_("Dm")], [K_("Lf")])
            if C == 128:
                TTo(R, "dve", Pk[0][:, :], Lf[:, :], cf[:, C_M16:C_M16 + 128], ALU.mult, [K_("Lf"), "cf"], [K_("Pk0")])
            else:
                TS(R, "dve", Pk[0][0:C, 0:C], Lf[0:C, 0:C], -1.0, None, ALU.mult, None, [K_("Lf")], [K_("Pk0")])
            TR(R, p_tr[0:C, 0:C], Pk[0][0:C, 0:C], ident[0:C, 0:C], [K_("Pk0"), "ident_b"], [pk_("p_tr")])
            CP(R, "act", PkT[0][0:C, 0:C], p_tr[0:C, 0:C], [pk_("p_tr")], [K_("PkT0")])
            TTo(R, "dve", TTf[0:C, 0:C], p_tr[0:C, 0:C], cf[0:C, C_ID:C_ID + C], ALU.add, [pk_("p_tr"), "cf"], [K_("TTf")])
            CP(R, "act", TTb[0:C, 0:C], TTf[0:C, 0:C], [K_("TTf")], [K_("TTb")])
            cur = 0
            for stp in range(nsteps):
                nx = 1 - cur
                MM(R, p_a[0:C, 0:C], PkT[cur][0:C, 0:C], Pk[cur][0:C, 0:C], [K_("Pk%d" % cur), K_("PkT%d" % cur)], [pk_("p_a")])
                CP(R, "act", Pk[nx][0:C, 0:C], p_a[0:C, 0:C], [pk_("p_a")], [K_("Pk%d" % nx)])
                if stp != nsteps - 1:
                    MM(R, p_b[0:C, 0:C], Pk[cur][0:C, 0:C], PkT[cur][0:C, 0:C], [K_("Pk%d" % cur), K_("PkT%d" % cur)], [pk_("p_b")])
                    CP(R, "dve", PkT[nx][0:C, 0:C], p_b[0:C, 0:C], [pk_("p_b")], [K_("PkT%d" % nx)])
                MM(R, p_c[0:C, 0:C], Pk[nx][0:C, 0:C], TTb[0:C, 0:C], [K_("Pk%d" % nx), K_("TTb")], [pk_("p_c")])
                TTo(R, "dve", TTf[0:C, 0:C], p_c[0:C, 0:C], TTf[0:C, 0:C], ALU.add, [pk_("p_c"), K_("TTf")], [K_("TTf")])
                CP(R, "act", TTb[0:C, 0:C], TTf[0:C, 0:C], [K_("TTf")], [K_("TTb")])
                cur = nx
            if C == 128:
                TR(R, p_tr[:, :], TTb[:, :], ident[:, :], [K_("TTb"), "ident_b"], [pk_("p_tr")])
                CP(R, "act", Tnf[:, :], p_tr[:, :], [pk_("p_tr")], [K_("Tnf")])
                CP(R, "dve", Tnb[:, :], p_tr[:, :], [pk_("p_tr")], [K_("Tnb")])
                for lv in range(3):
                    mc = C_MLO + lv * 128
                    TTo(R, "dve", Lo[:, :], Lf[:, :], cf[:, mc:mc + 128], ALU.mult, [K_("Lf"), "cf"], [K_("Lo")])
                    MM(R, p_a[:, :], Lo[:, :], TTb[:, :], [K_("Lo"), K_("TTb")], [pk_("p_a")])
                    CP(R, "act", W1T[:, :], p_a[:, :], [pk_("p_a")], [K_("W1T")])
                    MM(R, p_c[:, :], Tnb[:, :], W1T[:, :], [K_("Tnb"), K_("W1T")], [pk_("p_c")])
                    if lv < 2:
                        MM(R, p_b[:, :], W1T[:, :], Tnb[:, :], [K_("W1T"), K_("Tnb")], [pk_("p_b")])
                    TTo(R, "dve", TTf[:, :], TTf[:, :], p_c[:, :], ALU.subtract, [K_("TTf"), pk_("p_c")], [K_("TTf")])
                    CP(R, "act", TTb[:, :], TTf[:, :], [K_("TTf")], [K_("TTb")])
                    if lv < 2:
                        TTo(R, "dve", Tnf[:, :], Tnf[:, :], p_b[:, :], ALU.subtract, [K_("Tnf"), pk_("p_b")], [K_("Tnf")])
                        CP(R, "act", Tnb[:, :], Tnf[:, :], [K_("Tnf")], [K_("Tnb")])
            TS(R, "dve", vbeta[0:C, :], vb[bi][0:C, hs], g8[0:C, 4 + h:5 + h], None, ALU.mult, None, [kvb, kgb], [K_("vbeta")])
            TS(R, "dve", kbg[0:C, :], kb[bi][0:C, hs], bge[0:C, :], None, ALU.mult, None, [kkb, K_("bge")], [K_("kbg")])
            TS(R, "dve", kd[0:C, :], kb[bi][0:C, hs], kds[0:C, :], None, ALU.mult, None, [kkb, K_("kds")], [K_("kd")])
            MM(R, p_u[0:C, :], TTb[0:C, 0:C], vbeta[0:C, :], [K_("TTb"), K_("vbeta")], [pk_("p_u")])
            CP(R, "act", usb[0:C, :], p_u[0:C, :], [pk_("p_u")], [K_("usb")])
            MM(R, p_w[:, 0:C], kbg[0:C, :], TTb[0:C, 0:C], [K_("kbg"), K_("TTb")], [pk_("p_w")])
            CP(R, "act", wT[:, 0:C], p_w[:, 0:C], [pk_("p_w")], [K_("wT")])
            MM(R, p_qk[0:C, 0:C], kTt[bi][:, h, 0:C], qT[bi][:, h, 0:C], [kk_, kq], [pk_("p_qk")])
            TTo(R, "dve", qkT[0:C, 0:C], p_qk[0:C, 0:C], DTm[0:C, 0:C], ALU.mult, [pk_("p_qk"), K_("DTm")], [K_("qkT")])
            TTo(R, "dve", qgT[:, 0:C], qT[bi][:, h, 0:C], eGrow[:, 0:C], ALU.mult, [kq, K_("eGrow")], [K_("qgT")])
            sk, sbk = "g_s%d" % h, "g_sb%d" % h
            MM(R, p_ws[0:C, :], wT[:, 0:C], stb[h][:], [K_("wT"), sbk], [pk_("p_ws")])
            TTo(R, "dve", vnew[0:C, :], usb[0:C, :], p_ws[0:C, :], ALU.subtract, [K_("usb"), pk_("p_ws")], [K_("vnew")])
            MM(R, p_o[0:C, :], qgT[:, 0:C], stb[h][:], [K_("qgT"), sbk], [pk_("p_o")], start=True, stop=False)
            MM(R, p_o[0:C, :], qkT[0:C, 0:C], vnew[0:C, :], [K_("qkT"), K_("vnew")], [pk_("p_o")], start=False, stop=True)
            MM(R, p_s[:, :], kd[0:C, :], vnew[0:C, :], [K_("kd"), K_("vnew")], [pk_("p_s")])
            STT(R, "dve", st[h][:], st[h][:], gl[:, :], p_s[:, :], ALU.mult, ALU.add, [sk, K_("gl"), pk_("p_s")], [sk])
            CP(R, "act", stb[h][:], st[h][:], [sk], [sbk])
            ACT(R, junk[0:C, :], p_o[0:C, :], AF.Square, [pk_("p_o")], [K_("junk"), K_("ss")], accum_out=ss[0:C, :])
            ACT(R, ss[0:C, :], ss[0:C, :], AF.Sqrt, [K_("ss"), "eps6"], [K_("ss")], bias=k.eps6[0:C, :], scale=1.0 / 128.0)
            RCP(R, ss[0:C, :], ss[0:C, :], [K_("ss")], [K_("ss")])
            STT(R, "dve", ob[0:C, :], p_o[0:C, :], ss[0:C, :], k.dnw[0:C, :], ALU.mult, ALU.mult, [pk_("p_o"), K_("ss"), "dnw"], [K_("ob")])
            TTo(R, "dve", obt[0:C, hs], ob[0:C, :], szb[bi][0:C, hs], ALU.mult, [K_("ob"), ksz], ["obt%d" % h])

        cnt = [0]

        def chunk(C, tok0, bi, nsteps):
            kgb = "g_gb%d" % bi
            g8 = gbt[bi]
            MM(P, p_gc[0:C, :], cf[0:C, C_TRIU:C_TRIU + C], g8[0:C, 0:4], ["cf", kgb], ["p_gc"])
            CP(P, "dve", Gcol[0:C, :], p_gc[0:C, :], ["p_gc"], ["Gcol"])
            TS(P, "dve", nGcol[0:C, :], p_gc[0:C, :], -1.0, None, ALU.mult, None, ["p_gc"], ["nGcol"])
            recs = []
            for h in range(NH):
                R = Rec()
                head_ops(R, C, bi, h, nsteps)
                recs.append(R.items)
            for i in range(max(len(r) for r in recs)):
                for h in range(NH):
                    if i < len(recs[h]):
                        P.op(*recs[h][i])
            oi = cnt[0] % 2
            cnt[0] += 1
            for h in range(NH):
                TR(P, p_ot[:, h, 0:C], obt[0:C, h * 128:(h + 1) * 128], ident[0:C, 0:C], ["obt%d" % h, "ident_b"], ["p_ot"])
            CP(P, "act", obT[oi][:, :, 0:C], p_ot[:, :, 0:C], ["p_ot"], ["obT%d" % oi])
            P.dma("sp", k.d_obT[:, tok0:tok0 + C].rearrange("(h p) t -> p h t", p=128), obT[oi][:, :, 0:C], reads=["obT%d" % oi],
                  writes=["obT%d" % oi], tag="obT%d" % oi)

        def load(C, tok0, bi):
            kq, kk_, kkb, kvb, ksz, kgb = ["g_%s%d" % (n, bi) for n in ("qT", "kT", "kb", "vb", "sz", "gb")]
            P.dma("sp", qT[bi][:, :, 0:C], k.d_qbT[:, tok0:tok0 + C].rearrange("(h p) t -> p h t", p=128), writes=[kq], tag=kq)
            P.dma("pool", kTt[bi][:, :, 0:C], k.d_kbT[:, tok0:tok0 + C].rearrange("(h p) t -> p h t", p=128), writes=[kk_], tag=kk_)
            P.dma("sp", kb[bi][0:C, :], k.d_kb[tok0:tok0 + C, :], writes=[kkb], tag=kkb)
            P.dma("pool", vb[bi][0:C, :], k.d_vbn[tok0:tok0 + C, :], writes=[kvb], tag=kvb)
            P.dma("sp", szb[bi][0:C, :], k.d_sz[tok0:tok0 + C, :], writes=[ksz], tag=ksz)
            P.dma("pool", gbt[bi][0:C, :], k.d_gbeta[tok0:tok0 + C, :], writes=[kgb], tag=kgb)

        for h in range(NH):
            P.op("pool", lambda e, h=h: e.memset(st[h][:], 0.0), (), ["g_s%d" % h])
            P.op("pool", lambda e, h=h: e.memset(stb[h][:], 0.0), (), ["g_sb%d" % h])
        nch = S // 128
        load(128, 0, 0)
        for n in range(nch):
            if n + 1 < nch:
                load(128, (n + 1) * 128, (n + 1) % 2)
            chunk(128, n * 128, n % 2, 3)
        for h in range(NH):
            P.dma("sp", k.o_ndp[h], st[h][:], reads=["g_s%d" % h], writes=["g_s%d" % h], tag="ndp%d" % h)
        for s in range(4):
            bi = s % 2
            for h in range(NH):
                P.dma("pool", st[h][:], k.d_sdelta[s, h], writes=["g_s%d" % h], tag="ndp%d" % h)
                CP(P, "dve", stb[h][:], st[h][:], ["g_s%d" % h], ["g_sb%d" % h])
            load(16, S + s * 16, bi)
            chunk(16, S + s * 16, bi, 3)
            for h in range(NH):
                P.dma("sp", k.o_nds[s, h], st[h][:], reads=["g_s%d" % h], writes=["g_s%d" % h], tag="ndp%d" % h)
        P.emit()


def phase4(k):
    nc = k.nc
    alpha = (2.0 * 1) ** 0.25
    with ExitStack() as es:
        sb = lambda n, s, d=F32: es.enter_context(nc.sbuf_tensor(n, s, d))
        ps = lambda n, s, d=F32: es.enter_context(nc.psum_tensor(n, s, d))
        wpa = sb("wpa", [128, 4, 1024], BF16)
        wpb = sb("wpb", [128, 4, 1024], BF16)
        wo = sb("wo", [128, 8, 1024], BF16)
        wst = [sb("w4st%d" % i, [128, 4, 1024]) for i in range(2)]
        P = Prog(k.ctx)
        srcs = [(k.d_wpa, wpa, 0, 4), (k.d_wpb, wpb, 0, 4), (k.d_wout, wo, 0, 4), (k.d_wout, wo, 4, 4)]
        for i, (src, dst, k0, nk) in enumerate(srcs):
            w = wst[i % 2]
            wk = "w4st%d" % (i % 2)
            P.dma(_dq(i), w[:], src.rearrange("(kc p) n -> p kc n", p=128)[:, k0:k0 + nk, :], writes=[wk], tag=wk)
            for kc in range(nk):
                CP(P, "dve" if kc % 2 else "act", dst[:, k0 + kc, :], w[:, kc, :], [wk], ["w4"])
        oa = [sb("oa%d" % i, [128, 4, 512], BF16) for i in range(2)]
        obb = [sb("ob%d" % i, [128, 4, 512], BF16) for i in range(2)]
        ma = [sb("ma%d" % i, [128, 8, 512], BF16) for i in range(2)]
        mb = [sb("mb%d" % i, [128, 8, 512], BF16) for i in range(2)]
        mT = sb("mT", [128, 8, 512], BF16)
        t1 = sb("t1", [128, 512])
        t2 = sb("t2", [128, 512])
        xr = [sb("xr%d" % i, [128, 1024]) for i in range(2)]
        z = sb("z", [128, 1024])
        st = sb("st4", [128, 2, 6])
        mv = sb("mv4", [128, 2])
        rstd = sb("rstd4", [128, 1])
        yo = [sb("yo%d" % i, [128, 1024]) for i in range(2)]
        psA = ps("p4a", [128, 512])
        psB = ps("p4b", [128, 512])
        psY = [ps("p4y%d" % i, [128, 512]) for i in range(2)]
        for n_ in ("p4a", "p4b", "p4y0", "p4y1"):
            P.bank[n_] = n_
        tiles = [(i * 512, 4, 128, False) for i in range(S // 512)] + [(S, 1, 64, True)]
        for ti, (t0, nsub, R, smp) in enumerate(tiles):
            T = nsub * R
            b = ti % 2
            ks = ["oa%d" % b, "ob%d" % b, "ma%d" % b, "mb%d" % b]
            P.dma("sp", oa[b][:, :, 0:T], k.d_oaT[:, t0:t0 + T].rearrange("(c p) t -> p c t", p=128), writes=[ks[0]], tag=ks[0])
            P.dma("pool", obb[b][:, :, 0:T], k.d_obT[:, t0:t0 + T].rearrange("(c p) t -> p c t", p=128), writes=[ks[1]], tag=ks[1])
            P.dma("sp", ma[b][:, :, 0:T], k.d_smA[:, t0:t0 + T].rearrange("(c p) t -> p c t", p=128), writes=[ks[2]], tag=ks[2])
            P.dma("pool", mb[b][:, :, 0:T], k.d_smB[:, t0:t0 + T].rearrange("(c p) t -> p c t", p=128), writes=[ks[3]], tag=ks[3])
            for fc in range(8):
                fs = slice(fc * 128, (fc + 1) * 128)
                for kc in range(4):
                    MM(P, psA[:, 0:T], wpa[:, kc, fs], oa[b][:, kc, 0:T], ["w4", ks[0]], ["p4a"], start=(kc == 0), stop=(kc == 3))
                for kc in range(4):
                    MM(P, psB[:, 0:T], wpb[:, kc, fs], obb[b][:, kc, 0:T], ["w4", ks[1]], ["p4b"], start=(kc == 0), stop=(kc == 3))
                TTo(P, "dve", t1[:, 0:T], psA[:, 0:T], ma[b][:, fc, 0:T], ALU.mult, ["p4a", ks[2]], ["t1"])
                TTo(P, "dve", t2[:, 0:T], psB[:, 0:T], mb[b][:, fc, 0:T], ALU.mult, ["p4b", ks[3]], ["t2"])
                TTo(P, "dve", mT[:, fc, 0:T], t1[:, 0:T], t2[:, 0:T], ALU.add, ["t1", "t2"], ["mT"])
            for j in range(nsub):
                tok0 = t0 + j * R
                xs, xk = xr[j % 2], "xr%d" % (j % 2)
                P.dma(_dq(j), xs[0:R, :], k.d_xs if smp else k.d_xp[tok0:tok0 + R, :], writes=[xk], tag=xk)
                gate = k.gate_s if smp else k.gate_p
                for hf in range(2):
                    for fc in range(8):
                        MM(P, psY[hf][0:R, :], mT[:, fc, j * R:(j + 1) * R], wo[:, fc, hf * 512:(hf + 1) * 512], ["mT", "w4"], ["p4y%d" % hf],
                           start=(fc == 0), stop=(fc == 7))
                    TTo(P, "dve", z[0:R, hf * 512:(hf + 1) * 512], psY[hf][0:R, :], gate[0:R, hf * 512:(hf + 1) * 512], ALU.mult,
                        ["p4y%d" % hf, "gate_p", "gate_s"], ["z"])
                STT(P, "dve", z[0:R, :], xs[0:R, :], alpha, z[0:R, :], ALU.mult, ALU.add, [xk, "z"], ["z"])
                P.op("dve", lambda e, a=st[0:R, 0, :], b_=z[0:R, 0:512]: e.bn_stats(a, b_), ["z"], ["st40"])
                P.op("dve", lambda e, a=st[0:R, 1, :], b_=z[0:R, 512:1024]: e.bn_stats(a, b_), ["z"], ["st41"])
                P.op("dve", lambda e, a=mv[0:R, :], b_=st[0:R, :, :]: e.bn_aggr(a, b_), ["st40", "st41"], ["mv4"])
                ACT(P, rstd[0:R, :], mv[0:R, 1:2], AF.Sqrt, ["mv4", "eps5"], ["rstd4"], bias=k.eps5[0:R, :], scale=1.0)
                RCP(P, rstd[0:R, :], rstd[0:R, :], ["rstd4"], ["rstd4"])
                TS(P, "dve", z[0:R, :], z[0:R, :], mv[0:R, 0:1], rstd[0:R, :], ALU.subtract, ALU.mult, ["z", "mv4", "rstd4"], ["z"])
                y, yk = yo[j % 2], "yo%d" % (j % 2)
                TTo(P, "dve", y[0:R, :], z[0:R, :], k.lng[0:R, :], ALU.mult, ["z", "lng"], [yk])
                TTo(P, "dve", y[0:R, :], y[0:R, :], k.lnb[0:R, :], ALU.add, [yk, "lnb"], [yk])
                P.dma(_dq(j + 1), k.o_ys[:, :] if smp else k.o_yp[tok0:tok0 + R, :], y[0:R, :], reads=[yk], writes=[yk], tag=yk)
        P.emit()


def build_program():
    nc = bass.Bass("TRN2", target_bir_lowering=False)
    k = K()
    k.nc = nc
    k.ctx = Ctx(nc)
    din = lambda n, s: nc.dram_tensor(n, s, F32, kind="ExternalInput").ap()
    dout = lambda n, s: nc.dram_tensor(n, s, F32, kind="ExternalOutput").ap()
    k.d_xp = din("x_p", [S, D])
    k.d_xs = din("x_s", [NSM, D])
    k.d_c5T = din("c5T", [128, 8, 5])
    k.d_ck = din("ck", [4, PAST, 512])
    k.d_cv = din("cv", [4, PAST, 512])
    k.d_sconv = din("sconv", [4, 3, 1536])
    k.d_sdelta = din("sdelta", [4, 4, 128, 128])
    k.d_wada = din("w_ada", [D, 3 * D])
    k.d_bada = din("b_ada", [3 * D])
    k.d_badaT = din("b_adaT", [128, 24])
    k.d_win = din("w_in", [D, WIN])
    k.d_small = {}
    for n, s in (("lam_q1", 64), ("lam_k1", 64), ("lam_q2", 64), ("lam_k2", 64), ("subln_w", 128), ("a_log", 4), ("dt_bias", 4),
                 ("dn_norm_w", 128), ("ln_g", D), ("ln_b", D)):
        k.d_small[n] = din(n, [s])
    k.d_convwT = din("conv_wT", [128, 12, 4])
    k.d_wpa = din("w_pa", [512, D])
    k.d_wpb = din("w_pb", [512, D])
    k.d_wout = din("w_out", [D, D])
    k.d_rel = din("rel_table", [32, 4])
    k.d_consts = din("consts", [128, C_END])
    k.o_yp = dout("y_p", [S, D])
    k.o_ys = dout("y_s", [NSM, D])
    k.o_nkp = dout("nk_p", [S, 512])
    k.o_nvp = dout("nv_p", [S, 512])
    k.o_ncp = dout("nc_p", [3, 1536])
    k.o_ndp = dout("nd_p", [4, 128, 128])
    k.o_nks = dout("nk_s", [NSM, 512])
    k.o_nvs = dout("nv_s", [NSM, 512])
    k.o_ncs = dout("nc_s", [4, 3, 1536])
    k.o_nds = dout("nd_s", [4, 4, 128, 128])
    scr = lambda n, s, d=BF16: nc.dram_tensor(n, s, d, kind="ExternalOutput" if DEBUG_SCR else "Internal").ap()
    k.d_qT = scr("s_qT", [512, TT])
    k.d_kT = scr("s_kT", [512, TT])
    k.d_vb = scr("s_vb", [TT, 512])
    k.d_sg = scr("s_sg", [TT, 512])
    k.d_sgT = scr("s_sgT", [512, TT])
    k.d_qbT = scr("s_qbT", [512, TT])
    k.d_kbT = scr("s_kbT", [512, TT])
    k.d_kb = scr("s_kb", [TT, 512])
    k.d_vbn = scr("s_vbn", [TT, 512])
    k.d_sz = scr("s_sz", [TT, 512])
    k.d_gbeta = scr("s_gbeta", [TT, 8], F32)
    k.d_smA = scr("s_smA", [D, TT])
    k.d_smB = scr("s_smB", [D, TT])
    k.d_oaT = scr("s_oaT", [512, TT])
    k.d_obT = scr("s_obT", [512, TT])
    k.d_R_t = nc.dram_tensor("s_R", [4, 128, 384], F32, kind="Internal")
    k.d_R = k.d_R_t.ap()
    pb = lambda n, s, d=F32: nc.alloc_sbuf_tensor(n, s, d)
    k.cf = pb("cf", [128, C_END])
    k.ident_b = pb("ident_b", [128, 128], BF16)
    k.ones_b = pb("ones_b", [128, 128], BF16)
    k.modT = pb("modT", [128, 24, 5])
    k.gate_p = pb("gate_p", [128, D])
    k.gate_s = pb("gate_s", [64, D])
    k.lng = pb("lng", [128, D])
    k.lnb = pb("lnb", [128, D])
    k.nlam = pb("nlam", [128, 1])
    k.wsub = pb("wsub", [128, 128])
    k.wsubc = pb("wsubc", [128, 1])
    k.dnw = pb("dnw", [128, 128])
    k.convw = pb("convw", [128, 12, 4])
    k.dtb = pb("dtb", [128, 4])
    k.negA = pb("negA", [128, 4])
    k.eps5 = pb("eps5", [128, 1])
    k.eps6 = pb("eps6", [128, 1])
    k.pat = pb("pat", [128, 4, 2, 128], BF16)
    phs = [phase0, phase1, phase2, phase2s, phase3, phase4]
    for i, f in enumerate(phs):
        if i < PHASES and i not in SKIP:
            f(k)
    return nc


def _rel_bucket_np(rel):
    nb, max_exact = 16, 8
    n = np.abs(rel)
    lg = np.log(np.maximum(n, 1).astype(np.float32) / np.float32(max_exact)) / np.float32(math.log(128 / max_exact)) * np.float32(nb - max_exact)
    large = max_exact + lg.astype(np.float32).astype(np.int32)
    large = np.minimum(large, nb - 1)
    return np.where(rel > 0, nb, 0) + np.where(n < max_exact, n, large)


def _consts():
    c = np.zeros((128, C_END), np.float32)
    i = np.arange(128)[:, None]
    j = np.arange(128)[None, :]
    c[:, C_ID:C_ID + 128] = (i == j)
    c[:, C_TRIU:C_TRIU + 128] = (i <= j)
    c[:, C_MS:C_MS + 128] = np.where(i > j, 0.0, NEG)
    c[:, C_MDT:C_MDT + 128] = np.where(j >= i, 0.0, NEG)
    c[:, C_CM:C_CM + 128] = np.where((i // 64) <= (j // 64), 0.0, NEG)
    c[:, C_ONE:C_ONE + 128] = 1.0
    c[0, C_SEL:C_SEL + 128] = 1.0
    for p in range(64):
        c[1 + p // 16, C_SEL + 128 + p] = 1.0
    c[:, C_M16:C_M16 + 128] = np.where((i // 16) == (j // 16), -1.0, 0.0)
    for lv, bsz in enumerate((16, 32, 64)):
        c[:, C_MLO + lv * 128:C_MLO + (lv + 1) * 128] = ((i // (2 * bsz)) == (j // (2 * bsz))) & ((i // bsz) % 2 == 1) & ((j // bsz) % 2 == 0)
    rel = 127 - np.arange(384)
    bk = _rel_bucket_np(rel)
    for jj in range(384):
        c[bk[jj], C_OH + jj] += 1.0
        c[15, C_OH + jj] -= 1.0
    return c


_CACHE = {}


def kernel(**inp):
    f = lambda a: np.ascontiguousarray(np.asarray(a, dtype=np.float32))
    if "nc" not in _CACHE:
        _CACHE["nc"] = build_program()
    nc = _CACHE["nc"]
    consts = _consts()
    shared = {
        "w_ada": f(inp["w_ada"][0]), "b_ada": f(inp["b_ada"][0]),
        "b_adaT": f(np.asarray(inp["b_ada"][0]).reshape(24, 128).T),
        "w_in": f(inp["w_in"][0]),
        "conv_wT": f(np.asarray(inp["conv_w"][0]).reshape(4, 12, 128).transpose(2, 1, 0)),
        "w_pa": f(inp["w_pa"][0]), "w_pb": f(inp["w_pb"][0]), "w_out": f(inp["w_out"][0]),
        "rel_table": f(inp["rel_table"]), "consts": consts,
    }
    for n in ("lam_q1", "lam_k1", "lam_q2", "lam_k2", "subln_w", "a_log", "dt_bias", "dn_norm_w", "ln_g", "ln_b"):
        shared[n] = f(inp[n][0])
    xp, xsm = np.asarray(inp["x_prompt"]), np.asarray(inp["x_sample"])
    cp, cs = np.asarray(inp["c_prompt"]), np.asarray(inp["c_sample"])
    ck, cv = np.asarray(inp["cache_k"][0]), np.asarray(inp["cache_v"][0])
    sc, sd = np.asarray(inp["state_conv"][0]), np.asarray(inp["state_delta"][0])
    in_maps = []
    for c in range(8):
        sl = slice(4 * c, 4 * c + 4)
        c5 = np.concatenate([cp[c:c + 1], cs[sl]], axis=0)
        m = dict(shared)
        m["x_p"] = f(xp[c])
        m["x_s"] = f(xsm[sl].reshape(NSM, D))
        m["c5T"] = f(c5.reshape(5, 8, 128).transpose(2, 1, 0))
        m["ck"] = f(ck[sl].reshape(4, PAST, 512))
        m["cv"] = f(cv[sl].reshape(4, PAST, 512))
        m["sconv"] = f(sc[sl])
        m["sdelta"] = f(sd[sl])
        in_maps.append(m)
    res = run_bass_kernel_spmd(nc, in_maps, core_ids=list(range(8)), **({'trace': True} if TRACE else {}))
    if TRACE:
        print('EXEC_NS', res.exec_time_ns, flush=True)
    r = res.results
    global LAST
    LAST = r
    cat = lambda n: np.stack([np.asarray(r[c][n]) for c in range(8)], axis=0)
    y_p = cat("y_p")
    y_s = cat("y_s").reshape(32, 16, D)
    nk_p = cat("nk_p").reshape(1, 8, S, 4, 128)
    nv_p = cat("nv_p").reshape(1, 8, S, 4, 128)
    nc_p = cat("nc_p").reshape(1, 8, 3, 1536)
    nd_p = cat("nd_p").reshape(1, 8, 4, 128, 128)
    nk_s = cat("nk_s").reshape(1, 32, 16, 4, 128)
    nv_s = cat("nv_s").reshape(1, 32, 16, 4, 128)
    nc_s = cat("nc_s").reshape(1, 32, 3, 1536)
    nd_s = cat("nd_s").reshape(1, 32, 4, 128, 128)
    return (y_p, y_s, nk_p, nv_p, nc_p, nd_p, nk_s, nv_s, nc_s, nd_s)
```

```python
import math
from contextlib import ExitStack
import numpy as np
import concourse.bass as bass
import concourse.mybir as mybir
from concourse.bass_utils import run_bass_kernel_spmd

F32 = mybir.dt.float32
BF16 = mybir.dt.bfloat16
AF = mybir.ActivationFunctionType
ALU = mybir.AluOpType

S = 8192
D = 1024
NSM = 64
TT = S + NSM
NH = 4
PAST = 2048
QA, KA, VA, GA, DN, ZB, AB, MA, MB, WIN = 0, 512, 1024, 1536, 2048, 3584, 4096, 4104, 5128, 6152
NEG = -30000.0
ENGS = ("pe", "act", "dve", "pool", "sp")
C_ID, C_TRIU, C_MS, C_MDT, C_CM, C_ONE, C_SEL, C_OH, C_M16, C_MLO, C_END = 0, 128, 256, 384, 512, 640, 768, 960, 1344, 1472, 1856

PHASES = 99
CHECK = False
NAMES = {}
TRACE = False
DEBUG_SCR = False
LAST = None
SKIP = ()
EVE = None
DBG = 99


class Op:
    __slots__ = ("eng", "fn", "deps", "sig", "need", "dma", "tag", "idx", "line", "waits")


class Ctx:
    def __init__(self, nc):
        self.nc = nc
        self.sems = {e: nc.alloc_semaphore("s_" + e) for e in ENGS}
        self.cnt = {e: 0 for e in ENGS}
        self.tsems = {}
        self.tagcnt = {}
        self.tag_eng = {}


class Prog:
    def __init__(self, ctx, same_engine_sync=("act", "dve", "pool")):
        self.ctx = ctx
        self.nc = ctx.nc
        self.ops = []
        self.last_w = {}
        self.readers = {}
        self.same_sync = set(same_engine_sync)
        self.bank = {}

    def _add(self, eng, fn, reads, writes, dma=False, tag=None):
        if self.bank:
            extra = {self.bank[x] for x in list(reads) + list(writes) if x in self.bank}
            if extra:
                writes = list(writes) + [x for x in extra if x not in writes]
        o = Op()
        o.eng, o.fn, o.dma, o.tag, o.need, o.sig = eng, fn, dma, tag, False, None
        if TRACE:
            import sys as _sys
            f_ = _sys._getframe(1)
            while f_ is not None and f_.f_code.co_name in ("op", "dma", "_add", "MM", "TR", "ACT", "TS", "STT", "TTo", "CP", "RCP"):
                f_ = f_.f_back
            o.line = f_.f_lineno if f_ is not None else -1
        o.idx = len(self.ops)
        deps = set()
        for r in reads:
            w = self.last_w.get(r)
            if w is not None:
                deps.add(w)
        for w_ in writes:
            w = self.last_w.get(w_)
            if w is not None:
                deps.add(w)
            for rd in self.readers.get(w_, ()):
                deps.add(rd)
        o.deps = deps
        for r in reads:
            self.readers.setdefault(r, []).append(o.idx)
        for w_ in writes:
            self.last_w[w_] = o.idx
            self.readers[w_] = []
        self.ops.append(o)
        return o

    def op(self, eng, fn, reads=(), writes=()):
        return self._add(eng, fn, reads, writes)

    def dma(self, eng, out, in_, reads=(), writes=(), tag=None, **kw):
        eng = self.ctx.tag_eng.setdefault(tag, eng)
        return self._add(eng, lambda e: e.dma_start(out=out, in_=in_, **kw), reads, writes, dma=True, tag=tag)

    def _simulate(self, per, base_cnt, base_tag):
        ops = self.ops
        val = {("e", e): base_cnt[e] for e in ENGS}
        pc = {e: 0 for e in ENGS}
        progress = True
        while progress:
            progress = False
            for e in ENGS:
                while pc[e] < len(per[e]):
                    o = per[e][pc[e]]
                    ok = True
                    for d in o.deps:
                        y = ops[d]
                        if y.dma:
                            key = ("t", y.tag)
                        else:
                            if y.eng == e and not o.dma and e not in self.same_sync:
                                continue
                            key = ("e", y.eng)
                        if val.get(key, base_tag.get(key[1], 0) if key[0] == "t" else 0) < y.sig:
                            ok = False
                            break
                    if not ok:
                        break
                    if o.dma:
                        kk = ("t", o.tag)
                        val[kk] = val.get(kk, base_tag.get(o.tag, 0)) + 16
                        assert val[kk] == o.sig, (kk, val[kk], o.sig)
                    elif o.need:
                        val[("e", e)] += 1
                        assert val[("e", e)] == o.sig
                    pc[e] += 1
                    progress = True
        stuck = {e: (pc[e], len(per[e])) for e in ENGS if pc[e] < len(per[e])}
        print("SIM: ops", len(ops), "stuck", stuck, flush=True)
        assert not stuck

    def emit(self):
        nc, ctx, ops = self.nc, self.ctx, self.ops
        for o in ops:
            for d in o.deps:
                y = ops[d]
                if y.dma or y.eng != o.eng or o.dma or (y.eng in self.same_sync):
                    y.need = True
        base_cnt = dict(ctx.cnt)
        base_tag = dict(ctx.tagcnt)
        for o in ops:
            if o.dma:
                if o.tag not in ctx.tsems:
                    ctx.tsems[o.tag] = nc.alloc_semaphore("t%d" % len(ctx.tsems))
                    ctx.tagcnt[o.tag] = 0
                ctx.tagcnt[o.tag] += 16
                o.sig = ctx.tagcnt[o.tag]
            elif o.need:
                ctx.cnt[o.eng] += 1
                o.sig = ctx.cnt[o.eng]
        per = {e: [] for e in ENGS}
        for o in ops:
            per[o.eng].append(o)
        sems, tsems, tagcnt = ctx.sems, ctx.tsems, dict(ctx.tagcnt)
        if CHECK:
            self._simulate(per, base_cnt, base_tag)

        seen_all = {e: {} for e in ENGS}
        know = {}
        for o in ops:
            sd = seen_all[o.eng]
            waits = {}
            for d in o.deps:
                y = ops[d]
                if y.dma:
                    key = ("t", y.tag)
                else:
                    if y.eng == o.eng and not o.dma and o.eng not in self.same_sync:
                        continue
                    key = ("e", y.eng)
                if y.sig > waits.get(key, (0, None))[0]:
                    waits[key] = (y.sig, d)
            wl = []
            for key, (v, d) in waits.items():
                if sd.get(key, 0) >= v:
                    continue
                wl.append((key, v))
                sd[key] = v
                for k2, v2 in know.get(d, {}).items():
                    if v2 > sd.get(k2, 0):
                        sd[k2] = v2
            o.waits = wl
            if o.need or o.dma:
                know[o.idx] = dict(sd)

        def run(engname):
            def body(e):
                for o in per[engname]:
                    for key, v in o.waits:
                        e.wait_ge(tsems[key[1]] if key[0] == "t" else sems[key[1]], v)
                    ins = o.fn(e)
                    if TRACE:
                        NAMES[ins.ins.name] = o.line
                    if o.dma:
                        ins.then_inc(tsems[o.tag], 16)
                    elif o.need:
                        ins.then_inc(sems[o.eng], 1)
                done = set()
                for o in per[engname]:
                    if o.dma and o.tag not in done:
                        done.add(o.tag)
                        e.wait_ge(tsems[o.tag], tagcnt[o.tag])
            return body

        with nc.Block() as block:
            block.tensor(run("pe"))
            block.scalar(run("act"))
            block.vector(run("dve"))
            block.gpsimd(run("pool"))
            block.sync(run("sp"))


def MM(P, out, lhsT, rhs, r, w, start=True, stop=True):
    P.op("pe", lambda e: e.matmul(out, lhsT, rhs, start=start, stop=stop), r, w)


def TR(P, out, in_, ident, r, w):
    P.op("pe", lambda e: e.transpose(out, in_, ident), r, w)


def ACT(P, out, in_, func, r, w, **kw):
    P.op("act", lambda e: e.activation(out=out, in_=in_, func=func, **kw), r, w)


def TS(P, eng, out, in0, s1, s2, op0, op1, r, w):
    if s2 is None:
        P.op(eng, lambda e: e.tensor_scalar(out, in0, s1, None, op0=op0), r, w)
    else:
        P.op(eng, lambda e: e.tensor_scalar(out, in0, s1, s2, op0=op0, op1=op1), r, w)


def STT(P, eng, out, in0, sc, in1, op0, op1, r, w):
    eng = "dve"
    P.op(eng, lambda e: e.scalar_tensor_tensor(out=out, in0=in0, scalar=sc, in1=in1, op0=op0, op1=op1), r, w)


def TTo(P, eng, out, in0, in1, op, r, w):
    P.op(eng, lambda e: e.tensor_tensor(out=out, in0=in0, in1=in1, op=op), r, w)


def CP(P, eng, out, in_, r, w):
    if eng == "act":
        P.op(eng, lambda e: e.copy(out, in_), r, w)
    else:
        P.op(eng, lambda e: e.tensor_copy(out, in_), r, w)


def RCP(P, out, in_, r, w):
    P.op("dve", lambda e: e.reciprocal(out, in_), r, w)


class Rot:
    def __init__(self, items):
        self.items = items
        self.i = 0

    def next(self):
        it = self.items[self.i % len(self.items)]
        self.i += 1
        return it


class K:
    pass


def _dq(i):
    return "sp" if i % 2 == 0 else "pool"


def phase0(k):
    nc = k.nc
    P = Prog(k.ctx)
    with ExitStack() as es:
        sb = lambda n, s, d=F32: es.enter_context(nc.sbuf_tensor(n, s, d))
        ps = lambda n, s, d=F32: es.enter_context(nc.psum_tensor(n, s, d))
        c5t = sb("c5t", [128, 8, 5])
        scT = sb("scT", [128, 8, 5])
        wad = [sb("wad%d" % i, [128, 8, 512]) for i in range(2)]
        bad = sb("bad", [128, 24])
        bg5 = sb("bg5", [5, 1024])
        g5 = sb("g5", [5, 1024])
        lamv = sb("lamv", [128, 4, 64])
        lt = sb("lt", [128, 2, 64])
        ls = sb("ls", [128, 2])
        tab = sb("tab", [32, 4])
        lth = sb("lth", [32, 128])
        Rsb = sb("Rsb", [128, 384])
        pf = sb("pf", [128, 2, 128])
        alog = sb("alog", [128, 4])
        psm = ps("psm", [128, 512])[:, 0:120]
        psg = [ps("psg%d" % i, [128, 512]) for i in range(2)]
        psb = ps("psb", [128, 512])
        for n_ in ("psm", "psg0", "psg1", "psb"):
            P.bank[n_] = n_

        cf = k.cf
        P.dma("sp", cf[:], k.d_consts, writes=["cf"], tag="cf")
        P.dma("pool", c5t[:], k.d_c5T, writes=["c5t"], tag="c5t")
        P.dma("pool", bad[:], k.d_badaT, writes=["bad"], tag="bad")
        P.dma("pool", bg5[:], k.d_bada[2048:3072].partition_broadcast(5), writes=["bg5"], tag="bg5")
        P.dma("pool", k.convw[:], k.d_convwT, writes=["convw"], tag="convw")
        for i, nm in enumerate(("lam_q1", "lam_k1", "lam_q2", "lam_k2")):
            P.dma("pool", lamv[:, i, :], k.d_small[nm].partition_broadcast(128), writes=["lamv%d" % i], tag="lamv%d" % i)
        P.dma("pool", k.wsub[:], k.d_small["subln_w"].partition_broadcast(128), writes=["wsub"], tag="wsub")
        P.dma("pool", k.dnw[:], k.d_small["dn_norm_w"].partition_broadcast(128), writes=["dnw"], tag="dnw")
        P.dma("pool", k.lng[:], k.d_small["ln_g"].partition_broadcast(128), writes=["lng"], tag="lng")
        P.dma("pool", k.lnb[:], k.d_small["ln_b"].partition_broadcast(128), writes=["lnb"], tag="lnb")
        P.dma("pool", k.dtb[:], k.d_small["dt_bias"].partition_broadcast(128), writes=["dtb"], tag="dtb")
        P.dma("pool", alog[:], k.d_small["a_log"].partition_broadcast(128), writes=["alog"], tag="alog")
        P.dma("pool", tab[:], k.d_rel, writes=["tab"], tag="tab")
        CP(P, "dve", k.ident_b[:], cf[:, C_ID:C_ID + 128], ["cf"], ["ident_b"])
        CP(P, "dve", k.ones_b[:], cf[:, C_ONE:C_ONE + 128], ["cf"], ["ones_b"])
        P.op("pool", lambda e: e.memset(k.eps5[:], 1e-5), (), ["eps5"])
        P.op("pool", lambda e: e.memset(k.eps6[:], 1e-6), (), ["eps6"])
        TS(P, "dve", k.wsub[:], k.wsub[:], 0.8, None, ALU.mult, None, ["wsub"], ["wsub"])
        P.dma("pool", k.wsubc[:], k.d_small["subln_w"].rearrange("(p o) -> p o", o=1), writes=["wsubc"], tag="wsubc")
        TS(P, "dve", k.wsubc[:], k.wsubc[:], 0.8, None, ALU.mult, None, ["wsubc"], ["wsubc"])
        ACT(P, k.negA[:], alog[:], AF.Exp, ["alog"], ["negA"])
        TS(P, "dve", k.negA[:], k.negA[:], -1.0, None, ALU.mult, None, ["negA"], ["negA"])
        TTo(P, "dve", lt[:, 0, :], lamv[:, 0, :], lamv[:, 1, :], ALU.mult, ["lamv0", "lamv1"], ["lt0"])
        TTo(P, "dve", lt[:, 1, :], lamv[:, 2, :], lamv[:, 3, :], ALU.mult, ["lamv2", "lamv3"], ["lt1"])
        P.op("dve", lambda e: e.reduce_sum(ls[:], lt[:], axis=mybir.AxisListType.X), ["lt0", "lt1"], ["ls"])
        ACT(P, ls[:], ls[:], AF.Exp, ["ls"], ["ls"])
        TTo(P, "dve", k.nlam[:], ls[:, 1:2], ls[:, 0:1], ALU.subtract, ["ls"], ["nlam"])
        TS(P, "dve", k.nlam[:], k.nlam[:], -0.2, None, ALU.add, None, ["nlam"], ["nlam"])
        ACT(P, scT[:], c5t[:], AF.Silu, ["c5t"], ["scT"])
        wv = k.d_wada.rearrange("(kc p) n -> p kc n", p=128)
        for g in range(6):
            w = wad[g % 2]
            wk = "wad%d" % (g % 2)
            P.dma(_dq(g), w[:], wv[:, :, g * 512:(g + 1) * 512], writes=[wk], tag=wk)
            for fc in range(4):
                col = (g * 4 + fc) * 5
                for kc in range(8):
                    MM(P, psm[:, col:col + 5], w[:, kc, fc * 128:(fc + 1) * 128], scT[:, kc, :], [wk, "scT"], ["psm"],
                       start=(kc == 0), stop=(kc == 7))
            if g >= 4:
                for kc in range(8):
                    MM(P, psg[g - 4][0:5, :], scT[:, kc, :], w[:, kc, :], [wk, "scT"], ["psg%d" % (g - 4)],
                       start=(kc == 0), stop=(kc == 7))
        for fc in range(24):
            TS(P, "dve", k.modT[:, fc, :], psm[:, fc * 5:fc * 5 + 5], bad[:, fc:fc + 1],
               1.0 if 8 <= fc < 16 else 0.0, ALU.add, ALU.add, ["psm", "bad"], ["modT"])
        for hf in range(2):
            TTo(P, "dve", g5[:, hf * 512:(hf + 1) * 512], psg[hf][0:5, :], bg5[:, hf * 512:(hf + 1) * 512], ALU.add,
                ["psg%d" % hf, "bg5"], ["g5"])
        for hf in range(2):
            MM(P, psb[:, :], cf[0:5, C_SEL:C_SEL + 128], g5[:, hf * 512:(hf + 1) * 512], ["cf", "g5"], ["psb"])
            CP(P, "dve", k.gate_p[:, hf * 512:(hf + 1) * 512], psb[:, :], ["psb"], ["gate_p"])
            MM(P, psb[0:64, :], cf[0:5, C_SEL + 128:C_SEL + 192], g5[:, hf * 512:(hf + 1) * 512], ["cf", "g5"], ["psb"])
            CP(P, "dve", k.gate_s[:, hf * 512:(hf + 1) * 512], psb[0:64, :], ["psb"], ["gate_s"])
        for h in range(NH):
            TS(P, "dve", lth[:], cf[0:32, C_ONE:C_ONE + 128], tab[:, h:h + 1], None, ALU.mult, None, ["cf", "tab"], ["lth"])
            MM(P, psb[:, 0:384], lth[:], cf[0:32, C_OH:C_OH + 384], ["lth", "cf"], ["psb"])
            CP(P, "dve", Rsb[:], psb[:, 0:384], ["psb"], ["Rsb"])
            P.dma("sp", k.d_R[h], Rsb[:], reads=["Rsb"], writes=["dR"], tag="dR")
            base = h * 128 * 384
            P.dma("sp", pf[:, 0, :], bass.AP(k.d_R_t, base + 127, [[383, 128], [1, 128]]), reads=["dR"], writes=["pf"], tag="pf")
            P.dma("sp", pf[:, 1, :], bass.AP(k.d_R_t, base + 255, [[383, 128], [1, 128]]), reads=["dR"], writes=["pf"], tag="pf")
            TTo(P, "dve", k.pat[:, h, 0, :], pf[:, 0, :], cf[:, C_CM:C_CM + 128], ALU.add, ["pf", "cf"], ["pat"])
            CP(P, "dve", k.pat[:, h, 1, :], pf[:, 1, :], ["pf"], ["pat"])
        P.emit()


def phase1(k):
    nc = k.nc
    with ExitStack() as es0:
        winb = es0.enter_context(nc.sbuf_tensor("winb", [128, 8, WIN], BF16))
        P = Prog(k.ctx)
        with ExitStack() as es:
            wst = [es.enter_context(nc.sbuf_tensor("wst%d" % i, [128, 8, 512], F32)) for i in range(2)]
            wv = k.d_win.rearrange("(kc p) n -> p kc n", p=128)
            for g in range(13):
                c0 = g * 512
                cw = min(512, WIN - c0)
                w = wst[g % 2]
                wk = "wst%d" % (g % 2)
                P.dma(_dq(g), w[:, :, 0:cw], wv[:, :, c0:c0 + cw], writes=[wk], tag=wk)
                for kc in range(8):
                    CP(P, "dve" if kc % 2 == 0 else "act", winb[:, kc, c0:c0 + cw], w[:, kc, 0:cw], [wk], ["winb"])
            P.emit()
        P = Prog(k.ctx)
        with ExitStack() as es:
            sb = lambda n, s, d=F32: es.enter_context(nc.sbuf_tensor(n, s, d))
            ps = lambda n, s, d=F32: es.enter_context(nc.psum_tensor(n, s, d))
            xt = [sb("xt%d" % i, [128, 1024]) for i in range(2)]
            st = sb("st", [128, 2, 6])
            mv = sb("mv", [128, 2])
            rstd = sb("rstd", [128, 1])
            xn = sb("xn", [128, 4, 1024], BF16)
            hTs = [sb("hT%d" % i, [128, 8, 512], BF16) for i in range(2)]
            dnT = sb("dnT", [128, 12, 515])
            dnS = sb("dnS", [128, 12, 4, 19])
            acc = sb("acc", [128, 512])
            yv = sb("yv", [128, 512])
            yb = [sb("yb%d" % i, [128, 512], BF16) for i in range(2)]
            sq = sb("sq", [128, 512], BF16)
            rn = sb("rn", [128, 512])
            fm = Rot([(sb("fm%d" % i, [128, 512], BF16), "fm%d" % i) for i in range(4)])
            tf = Rot([(sb("tf%d" % i, [128, 512]), "tf%d" % i) for i in range(3)])
            tb = Rot([(sb("tb%d" % i, [128, 512], BF16), "tb%d" % i) for i in range(4)])
            gb = sb("gb", [128, 4, 8])
            t4 = sb("t4", [128, 4])
            psA = Rot([(ps("psA%d" % i, [128, 512]), "psA%d" % i) for i in range(3)])
            psB = psA
            psn = ps("psn", [128, 512])
            psT = [ps("psT%d" % i, [128, 2, 4, 128], BF16) for i in range(2)]
            psX = Rot([(ps("psX%d" % i, [128, 8, 128], BF16)[:, 0:4, :], "psX%d" % i) for i in range(1)])
            ps8 = ps("ps8", [128, 512])[:, 0:8]
            for n_ in ("psA0", "psA1", "psA2", "psn", "psT0", "psT1", "psX0", "ps8"):
                P.bank[n_] = n_
            ident = k.ident_b
            P.op("pool", lambda e: e.memset(dnT[:, :, 0:3], 0.0), (), ["dnT"])
            for s in range(4):
                for r_ in range(3):
                    P.dma("sp", dnS[:, :, s, r_], k.d_sconv[s, r_].rearrange("(i p) -> p i", p=128), writes=["dnT"], tag="dnSin",
                          allow_slow_non_contiguous=True)
            tiles = [(i * 512, 4, 128, False) for i in range(S // 512)] + [(S, 1, 64, True)]
            dq = 0

            def tile_gen(ti, t0, nsub, R, smp):
                nonlocal dq
                hT = hTs[ti % 2]
                hk = "hT%d" % (ti % 2)
                T = nsub * R
                for j in range(nsub):
                    xs = xt[j % 2]
                    xk = "xt%d" % (j % 2)
                    src = k.d_xs if smp else k.d_xp[t0 + j * R:t0 + (j + 1) * R, :]
                    P.dma(_dq(j), xs[0:R, :], src, writes=[xk], tag=xk)
                    P.op("dve", lambda e, a=st[0:R, 0, :], b_=xs[0:R, 0:512]: e.bn_stats(a, b_), [xk], ["st0"])
                    P.op("dve", lambda e, a=st[0:R, 1, :], b_=xs[0:R, 512:1024]: e.bn_stats(a, b_), [xk], ["st1"])
                    P.op("dve", lambda e, a=mv[0:R, :], b_=st[0:R, :, :]: e.bn_aggr(a, b_), ["st0", "st1"], ["mv"])
                    ACT(P, rstd[0:R, :], mv[0:R, 1:2], AF.Sqrt, ["mv", "eps5"], ["rstd"], bias=k.eps5[0:R, :], scale=1.0)
                    RCP(P, rstd[0:R, :], rstd[0:R, :], ["rstd"], ["rstd"])
                    TS(P, "dve", xn[0:R, j, :], xs[0:R, :], mv[0:R, 0:1], rstd[0:R, :], ALU.subtract, ALU.mult,
                       [xk, "mv", "rstd"], ["xn%d" % j])
                yield
                for fc in range(8):
                    px, pk = psX.next()
                    for j in range(nsub):
                        TR(P, px[:, j, 0:R], xn[0:R, j, fc * 128:(fc + 1) * 128], ident[0:R, 0:R], ["xn%d" % j, "ident_b"], [pk])
                    if not smp:
                        ACT(P, hT[:, fc, :], px.rearrange("p a b -> p (a b)"), AF.Identity, [pk, "modT"], [hk],
                            scale=k.modT[:, 8 + fc, 0:1], bias=k.modT[:, fc, 0:1])
                    else:
                        for s in range(4):
                            ACT(P, hT[:, fc, s * 16:(s + 1) * 16], px[:, 0, s * 16:(s + 1) * 16], AF.Identity, [pk, "modT"], [hk],
                                scale=k.modT[:, 8 + fc, 1 + s:2 + s], bias=k.modT[:, fc, 1 + s:2 + s])
                if False:
                    pass
                yield
                fmlist = ([("q", QA, i) for i in range(4)] + [("k", KA, i) for i in range(4)] + ([] if smp else [("g", GA, i) for i in range(4)])
                          + [("dn", DN, i) for i in range(12)]
                          + [("ma", MA, i) for i in range(8)] + [("mb", MB, i) for i in range(8)])
                for n_, (kind, cb, i) in enumerate(fmlist):
                    if n_ == len(fmlist) // 2:
                        yield
                    pa, pk = psA.next()
                    col = cb + i * 128
                    for kc in range(8):
                        MM(P, pa[:, 0:T], winb[:, kc, col:col + 128], hT[:, kc, 0:T], ["winb", hk], [pk], start=(kc == 0), stop=(kc == 7))
                    if kind == "dn":
                        if not smp:
                            CP(P, "dve" if i % 2 else "act", dnT[:, i, 3:3 + T], pa[:, 0:T], [pk], ["dnT"])
                        else:
                            CP(P, "dve", dnS[:, i, :, 3:19], pa[:, 0:64].rearrange("p (s c) -> p s c", c=16), [pk], ["dnT"])
                        continue
                    f, fk = fm.next()
                    if kind == "q":
                        ACT(P, f[:, 0:T], pa[:, 0:T], AF.Copy, [pk], [fk], scale=0.125)
                        dst = k.d_qT
                    elif kind == "k":
                        CP(P, "dve", f[:, 0:T], pa[:, 0:T], [pk], [fk])
                        dst = k.d_kT
                    elif kind == "g":
                        ACT(P, f[:, 0:T], pa[:, 0:T], AF.Silu, [pk], [fk])
                        dst = k.d_sgT
                    else:
                        ACT(P, f[:, 0:T], pa[:, 0:T], AF.Sigmoid, [pk], [fk])
                        dst = k.d_smA if kind == "ma" else k.d_smB
                    dq += 1
                    P.dma(_dq(dq), dst[i * 128:(i + 1) * 128, t0:t0 + T], f[:, 0:T], reads=[fk], writes=[fk], tag=fk)
                yield
                units = []

                def tok_unit(j, kind, cb, R=R, t0=t0, smp=smp):
                    nonlocal dq
                    tok0 = t0 + j * R
                    hsl = slice(j * R, (j + 1) * R)
                    if True:
                        pb, pk = psB.next()
                        for kc in range(8):
                            MM(P, pb[0:R, :], hT[:, kc, hsl], winb[:, kc, cb:cb + 512], ["winb", hk], [pk], start=(kc == 0), stop=(kc == 7))
                        if kind in ("k", "v"):
                            f, fk = tf.next()
                            CP(P, "dve" if kind == "k" else "act", f[0:R, :], pb[0:R, :], [pk], [fk])
                            if smp:
                                dst = (k.o_nks if kind == "k" else k.o_nvs)[:, :]
                            else:
                                dst = (k.o_nkp if kind == "k" else k.o_nvp)[tok0:tok0 + R, :]
                            dq += 1
                            if kind == "v":
                                b, bk = tb.next()
                                CP(P, "dve", b[0:R, :], f[0:R, :], [fk], [bk])
                                P.dma(_dq(dq + 1), k.d_vb[tok0:tok0 + R, :], b[0:R, :], reads=[bk], writes=[bk], tag=bk)
                            P.dma(_dq(dq), dst, f[0:R, :], reads=[fk], writes=[fk], tag=fk)
                        else:
                            b, bk = tb.next()
                            ACT(P, b[0:R, :], pb[0:R, :], AF.Silu, [pk], [bk])
                            dq += 1
                            P.dma(_dq(dq), (k.d_sg if kind == "g" else k.d_sz)[tok0:tok0 + R, :], b[0:R, :], reads=[bk], writes=[bk], tag=bk)

                def ab_unit(j, R=R, t0=t0):
                    tok0 = t0 + j * R
                    hsl = slice(j * R, (j + 1) * R)
                    for kc in range(8):
                        MM(P, ps8[0:R, :], hT[:, kc, hsl], winb[:, kc, AB:AB + 8], ["winb", hk], ["ps8"], start=(kc == 0), stop=(kc == 7))
                    TTo(P, "dve", t4[0:R, :], ps8[0:R, 0:4], k.dtb[0:R, :], ALU.add, ["ps8", "dtb"], ["t4"])
                    ACT(P, t4[0:R, :], t4[0:R, :], AF.Exp, ["t4"], ["t4"])
                    ACT(P, t4[0:R, :], t4[0:R, :], AF.Ln, ["t4"], ["t4"], bias=1.0, scale=1.0)
                    TTo(P, "dve", gb[0:R, j, 0:4], t4[0:R, :], k.negA[0:R, :], ALU.mult, ["t4", "negA"], ["gb"])
                    ACT(P, gb[0:R, j, 4:8], ps8[0:R, 4:8], AF.Sigmoid, ["ps8"], ["gb"])
                    P.dma("sp", k.d_gbeta[tok0:tok0 + R, :], gb[0:R, j, :], reads=["gb"], writes=["gb"], tag="gb")

                for j in range(nsub):
                    for kind, cb in ((("k", KA), ("v", VA), ("g", GA), ("z", ZB)) if smp else (("k", KA), ("v", VA), ("z", ZB))):
                        units.append(lambda j=j, kind=kind, cb=cb: tok_unit(j, kind, cb))
                    units.append(lambda j=j: ab_unit(j))
                for i in range(12):
                    cw = k.convw
                    if not smp:
                        xv = lambda o: dnT[:, i, o:o + T]
                        av = acc[:, 0:T]
                    else:
                        xv = lambda o: dnS[:, i, :, o:o + 16]
                        av = acc[:, 0:64].rearrange("p (s c) -> p s c", c=16)
                    TS(P, "dve", av, xv(0), cw[:, i, 0:1], None, ALU.mult, None, ["dnT", "convw"], ["acc"])
                    for jj in (1, 2):
                        STT(P, "pool", av, xv(jj), cw[:, i, jj:jj + 1], av, ALU.mult, ALU.add, ["dnT", "convw", "acc"], ["acc"])
                    STT(P, "dve", av, xv(3), cw[:, i, 3:4], av, ALU.mult, ALU.add, ["dnT", "convw", "acc"], ["acc"])
                    if units:
                        units.pop(0)()
                    if i < 8:
                        ACT(P, yv[:, 0:T], acc[:, 0:T], AF.Silu, ["acc"], ["yv"])
                        ACT(P, sq[:, 0:T], yv[:, 0:T], AF.Square, ["yv"], ["sq"])
                        MM(P, psn[:, 0:T], k.ones_b[:], sq[:, 0:T], ["ones_b", "sq"], ["psn"])
                        ACT(P, rn[:, 0:T], psn[:, 0:T], AF.Sqrt, ["psn", "eps6"], ["rn"], bias=k.eps6[:, :], scale=1.0)
                        RCP(P, rn[:, 0:T], rn[:, 0:T], ["rn"], ["rn"])
                        y, yk = yb[i % 2], "yb%d" % (i % 2)
                        STT(P, "dve", y[:, 0:T], yv[:, 0:T], (128.0 ** -0.5) if i < 4 else 1.0, rn[:, 0:T], ALU.mult, ALU.mult,
                            ["yv", "rn"], [yk])
                        dq += 1
                        if DBG >= 3.4:
                            P.dma(_dq(dq), (k.d_qbT if i < 4 else k.d_kbT)[(i % 4) * 128:(i % 4 + 1) * 128, t0:t0 + T], y[:, 0:T],
                                  reads=[yk], writes=[yk], tag=yk)
                    else:
                        y, yk = yb[i % 2], "yb%d" % (i % 2)
                        ACT(P, y[:, 0:T], acc[:, 0:T], AF.Silu, ["acc"], [yk])
                    if units:
                        units.pop(0)()
                    if i >= 4 and DBG >= 3.6:
                        for j in range(nsub):
                            TR(P, psT[j // 2][0:R, j % 2, i % 4, :], y[:, j * R:(j + 1) * R], ident[:, :], [yk, "ident_b"], ["psT%d" % (j // 2)])
                        if i % 4 == 3 and DBG >= 3.7:
                            for j in range(nsub):
                                b, bk = tb.next()
                                CP(P, EVE if EVE else ("dve" if j % 2 else "act"), b[0:R, :], psT[j // 2][0:R, j % 2, :, :].rearrange("p a b -> p (a b)"),
                                   ["psT%d" % (j // 2)], [bk])
                                dq += 1
                                tok0 = t0 + j * R
                                if DBG >= 3.8:
                                  P.dma(_dq(dq), (k.d_kb if i == 7 else k.d_vbn)[tok0:tok0 + R, :], b[0:R, :], reads=[bk], writes=[bk], tag=bk)
                if False:
                    pass
                while units:
                    units.pop(0)()
                if not smp:
                    if t0 + T == S:
                        for r_ in range(3):
                            P.dma("sp", k.o_ncp[r_].rearrange("(i p) -> p i", p=128), dnT[:, :, T + r_], reads=["dnT"], tag="nco",
                                  allow_slow_non_contiguous=True)
                    else:
                        CP(P, "pool", dnT[:, :, 0:3], dnT[:, :, T:T + 3], ["dnT"], ["dnT"])
                else:
                    for s in range(4):
                        for r_ in range(3):
                            P.dma("sp", k.o_ncs[s, r_].rearrange("(i p) -> p i", p=128), dnS[:, :, s, 16 + r_], reads=["dnT"], tag="nco",
                                  allow_slow_non_contiguous=True)

            gens = [tile_gen(ti, *tl) for ti, tl in enumerate(tiles)]

            def adv(g):
                try:
                    next(g)
                except StopIteration:
                    pass
            adv(gens[0])
            adv(gens[0])
            for i_ in range(len(gens)):
                adv(gens[i_])
                if i_ + 1 < len(gens):
                    adv(gens[i_ + 1])
                adv(gens[i_])
                if i_ + 1 < len(gens):
                    adv(gens[i_ + 1])
                adv(gens[i_])
            P.emit()


def attn_post(P, k, h, R, O1s, O2s, sgt, sgk, oaT_dst, wk, pst, tag):
    nc = k.nc
    for qs, ((o1, k1), (o2, k2)) in enumerate(zip(O1s, O2s)):
        r1, r2, o1s, od, ss, junk, oab, oat = wk
        RCP(P, r1[0:R, :], o1[:, 128:129], [k1], ["ap_r1"])
        TS(P, "dve", o1s[0:R, :], o1[:, 0:128], r1[0:R, :], None, ALU.mult, None, [k1, "ap_r1"], ["ap_o1s"])
        RCP(P, r2[0:R, :], o2[:, 128:129], [k2], ["ap_r2"])
        TTo(P, "dve", r2[0:R, :], r2[0:R, :], k.nlam[0:R, :], ALU.mult, ["ap_r2", "nlam"], ["ap_r2"])
        STT(P, "dve", od[0:R, :], o2[:, 0:128], r2[0:R, :], o1s[0:R, :], ALU.mult, ALU.add, [k2, "ap_r2", "ap_o1s"], ["ap_od"])
        ACT(P, junk[0:R, :], od[0:R, :], AF.Square, ["ap_od"], ["ap_junk", "ap_ss"], accum_out=ss[0:R, :])
        ACT(P, ss[0:R, :], ss[0:R, :], AF.Sqrt, ["ap_ss", "eps5"], ["ap_ss"], bias=k.eps5[0:R, :], scale=1.0 / 128.0)
        RCP(P, ss[0:R, :], ss[0:R, :], ["ap_ss"], ["ap_ss"])
        STT(P, "dve", od[0:R, :], od[0:R, :], ss[0:R, :], k.wsub[0:R, :], ALU.mult, ALU.mult, ["ap_od", "ap_ss", "wsub"], ["ap_od"])
        TTo(P, "dve", oab[0:R, :], od[0:R, :], sgt(qs), ALU.mult, ["ap_od", sgk], ["ap_oab"])
        TR(P, pst[:, 0:R], oab[0:R, :], k.ident_b[0:R, 0:R], ["ap_oab", "ident_b"], ["pst"])
        CP(P, "dve", oat[:, qs * R:(qs + 1) * R], pst[:, 0:R], ["pst"], [tag])


def phase2(k):
    nc = k.nc
    P = Prog(k.ctx)
    with ExitStack() as es:
        sb = lambda n, s, d=F32: es.enter_context(nc.sbuf_tensor(n, s, d))
        ps = lambda n, s, d=F32: es.enter_context(nc.psum_tensor(n, s, d))
        KT = [sb("KT%d" % i, [128, S], BF16) for i in range(2)]
        QT = [sb("QT%d" % i, [128, S], BF16) for i in range(2)]
        V = [sb("V%d" % i, [128, S // 128, 128], BF16) for i in range(2)]
        pT = Rot([(sb("pT%d" % i, [128, 512], BF16), "pT%d" % i) for i in range(8)])
        sgt = [sb("sgt%d" % i, [128, 512], BF16) for i in range(2)]
        oat = [sb("oat%d" % i, [128, 512], BF16) for i in range(2)]
        o1s = [sb("o1s%d" % i, [128, 512]) for i in range(2)]
        o2s = [sb("o2s%d" % i, [128, 512]) for i in range(2)]
        r1 = sb("a_r1", [128, 512])
        r2 = sb("a_r2", [128, 512])
        od = sb("a_od", [128, 512])
        sq = sb("a_sq", [128, 512], BF16)
        rs = sb("a_rs", [128, 512])
        psS = Rot([(ps("psS%d" % i, [128, 512]), "psS%d" % i) for i in range(4)])
        psOT = [ps("psOT%d" % c, [128, 512]) for c in range(2)]
        psRS = [ps("psRS%d" % c, [128, 512]) for c in range(2)]
        for n_ in ("psS0", "psS1", "psS2", "psS3", "psOT0", "psOT1", "psRS0", "psRS1"):
            P.bank[n_] = n_
        pending = []
        qbi = 0
        for h in range(NH):
            b = h % 2
            P.dma("sp", KT[b][:], k.d_kT[h * 128:(h + 1) * 128, 0:S], writes=["KT%d" % b], tag="KT%d" % b)
            P.dma("pool", QT[b][:], k.d_qT[h * 128:(h + 1) * 128, 0:S], writes=["QT%d" % b], tag="QT%d" % b)
            P.dma("sp", V[b][:], k.d_vb[0:S, h * 128:(h + 1) * 128].rearrange("(n p) e -> p n e", p=128),
                  writes=["V%d" % b], tag="V%d" % b)
            kt, qt, v = KT[b], QT[b], V[b]
            rk = ["KT%d" % b, "QT%d" % b]
            for Qb in range(S // 512):
                sb_i = qbi % 2
                qbi += 1
                P.dma("pool", sgt[sb_i][:], k.d_sgT[h * 128:(h + 1) * 128, Qb * 512:(Qb + 1) * 512], writes=["sgt%d" % sb_i], tag="sgt%d" % sb_i)
                nkb = 4 * Qb + 4

                def qk(kb, c):
                    cs = slice(c * 64, (c + 1) * 64)
                    j = kb - 4 * Qb
                    col0 = max(j, 0) * 128
                    pS, pk = psS.next()
                    needb = j >= -1
                    MM(P, pS[:, col0:512], kt[cs, kb * 128:(kb + 1) * 128], qt[cs, Qb * 512 + col0:(Qb + 1) * 512], rk, [pk],
                       start=True, stop=not needb)
                    if needb:
                        lst = [(qs, qs - j) for qs in range(4) if (qs - j) in (0, 1)]
                        for n_, (qs, d) in enumerate(lst):
                            MM(P, pS[:, qs * 128:(qs + 1) * 128], k.ident_b[:], k.pat[:, h, d, :], ["ident_b", "pat"], [pk],
                               start=False, stop=(n_ == len(lst) - 1))
                    p_, ptk = pT.next()
                    ACT(P, p_[:, col0:512], pS[:, col0:512], AF.Exp, [pk], [ptk])
                    return (kb, col0, p_, ptk)

                def pv(infos):
                    kb, col0 = infos[0][0], infos[0][1]
                    last = (kb == nkb - 1)
                    for c in range(2):
                        MM(P, psOT[c][:, col0:512], v[:, kb, :], infos[c][2][:, col0:512], ["V%d" % b, infos[c][3]], ["psOT%d" % c],
                           start=(kb == 0), stop=last)
                    for c in range(2):
                        MM(P, psRS[c][:, col0:512], k.ones_b[:], infos[c][2][:, col0:512], ["ones_b", infos[c][3]], ["psRS%d" % c],
                           start=(kb == 0), stop=last)

                fifo = []
                for kb in range(nkb):
                    fifo.append((qk(kb, 0), qk(kb, 1)))
                    if len(fifo) > 1:
                        pv(fifo.pop(0))
                    if kb == 2 and pending:
                        pending.pop()()
                while fifo:
                    pv(fifo.pop(0))
                o1, o2 = o1s[sb_i], o2s[sb_i]
                RCP(P, r1[:], psRS[0][:], ["psRS0"], ["a_r1"])
                TTo(P, "dve", o1[:], psOT[0][:], r1[:], ALU.mult, ["psOT0", "a_r1"], ["o1s%d" % sb_i])
                RCP(P, r2[:], psRS[1][:], ["psRS1"], ["a_r2"])
                TTo(P, "dve", o2[:], psOT[1][:], r2[:], ALU.mult, ["psOT1", "a_r2"], ["o2s%d" % sb_i])

                def post(h=h, Qb=Qb, sb_i=sb_i, o1=o1, o2=o2):
                    ot, otk = oat[sb_i], "oat%d" % sb_i
                    STT(P, "dve", od[:], o2[:], k.nlam[:, :], o1[:], ALU.mult, ALU.add, ["o2s%d" % sb_i, "o1s%d" % sb_i, "nlam"], ["a_od"])
                    ACT(P, sq[:], od[:], AF.Square, ["a_od"], ["a_sq"])
                    pL, plk = psS.next()
                    MM(P, pL[:], k.ones_b[:], sq[:], ["ones_b", "a_sq"], [plk])
                    ACT(P, rs[:], pL[:], AF.Sqrt, [plk, "eps5"], ["a_rs"], bias=k.eps5[:, :], scale=1.0 / 128.0)
                    RCP(P, rs[:], rs[:], ["a_rs"], ["a_rs"])
                    TTo(P, "dve", od[:], od[:], rs[:], ALU.mult, ["a_od", "a_rs"], ["a_od"])
                    STT(P, "dve", ot[:], od[:], k.wsubc[:, :], sgt[sb_i][:], ALU.mult, ALU.mult, ["a_od", "wsubc", "sgt%d" % sb_i], [otk])
                    P.dma("sp", k.d_oaT[h * 128:(h + 1) * 128, Qb * 512:(Qb + 1) * 512], ot[:], reads=[otk], writes=[otk], tag=otk)
                if pending:
                    pending.pop()()
                pending.append(post)
        while pending:
            pending.pop()()
        P.emit()


def phase2s(k):
    nc = k.nc
    NBK = PAST // 128
    P = Prog(k.ctx)
    with ExitStack() as es:
        sb = lambda n, s, d=F32: es.enter_context(nc.sbuf_tensor(n, s, d))
        ps = lambda n, s, d=F32: es.enter_context(nc.psum_tensor(n, s, d))
        kfs = [sb("kf%d" % i, [128, NBK, 512]) for i in range(2)]
        kbf = sb("kbf", [128, NBK, 512], BF16)
        KTs = sb("KTs", [128, NBK, 4, 128], BF16)
        vfs = [sb("vf0", [128, NBK, 512])] * 2
        Vs = sb("Vs", [128, NBK, 4, 129], BF16)
        Vn = sb("Vn", [16, 4, 129], BF16)
        Qblk = sb("Qblk", [128, 4, 32], BF16)
        KTn = sb("KTn", [128, 4, 16], BF16)
        pTs = sb("pTs", [128, NBK * 32], BF16)
        pTn = sb("pTn", [16, 32], BF16)
        sgs = sb("sgs", [16, 512], BF16)
        oat = sb("oats", [128, 16], BF16)
        wk = (sb("s_r1", [128, 1]), sb("s_r2", [128, 1]), sb("s_o1s", [128, 128]), sb("s_od", [128, 128]), sb("s_ss", [128, 1]),
              sb("s_junk", [128, 128]), sb("s_oab", [128, 128], BF16), oat)
        psS = ps("psS", [128, 512])[:, 0:NBK * 32]
        psN = ps("psN", [128, 512])[0:16, 0:32]
        psO = [ps("psOs%d" % c, [128, 512])[0:16, 0:129] for c in range(2)]
        psK = Rot([(ps("psK%d" % i, [128, 8, 128], BF16)[:, 0:4, :], "psK%d" % i) for i in range(2)])
        pst = ps("pst2", [128, 1024], BF16)[:, 0:128]
        for n_ in ("psS", "psN", "psOs0", "psOs1", "psK0", "psK1", "pst"):
            P.bank[n_] = n_
        P.op("pool", lambda e: e.memset(Qblk[:], 0.0), (), ["Qblk"])
        P.op("pool", lambda e: e.memset(Vs[:, :, :, 128:129], 1.0), (), ["Vs"])
        P.op("pool", lambda e: e.memset(Vn[:, :, 128:129], 1.0), (), ["Vn"])
        def ldcache(s_):
            P.dma("sp", kfs[s_ % 2][:], k.d_ck[s_].rearrange("(n p) e -> p n e", p=128), writes=["kf%d" % (s_ % 2)], tag="kf%d" % (s_ % 2))

        def ldv(s_):
            P.dma("pool", vfs[0][:], k.d_cv[s_].rearrange("(n p) e -> p n e", p=128), writes=["vf0"], tag="vf0")
        ldcache(0)
        ldv(0)
        for s in range(4):
            tk0 = S + s * 16
            if s + 1 < 4:
                ldcache(s + 1)
            kf, vf = kfs[s % 2], vfs[s % 2]
            kfk, vfk = "kf%d" % (s % 2), "vf0"
            qv = k.d_qT[:, tk0:tk0 + 16].rearrange("(h p) t -> p h t", p=128)
            P.dma("sp", Qblk[0:64, :, 0:16], qv[0:64], writes=["Qblk"], tag="Qblk")
            P.dma("sp", Qblk[64:128, :, 16:32], qv[64:128], writes=["Qblk"], tag="Qblk")
            P.dma("pool", KTn[:, :, :], k.d_kT[:, tk0:tk0 + 16].rearrange("(h p) t -> p h t", p=128), writes=["KTn"], tag="KTn")
            P.dma("pool", Vn[:, :, 0:128], k.d_vb[tk0:tk0 + 16, :].rearrange("p (h e) -> p h e", e=128), writes=["Vn"], tag="Vn")
            P.dma("pool", sgs[:], k.d_sg[tk0:tk0 + 16, :], writes=["sgs"], tag="sgs")
            for n in range(NBK):
                CP(P, "dve" if n % 2 else "act", kbf[:, n, :], kf[:, n, :], [kfk], ["kbf"])
                CP(P, "act" if n % 2 else "dve", Vs[:, n, :, 0:128], vf[:, n, :].rearrange("p (h e) -> p h e", e=128), [vfk], ["Vs"])
            if s + 1 < 4:
                ldv(s + 1)
            for n in range(NBK):
                pk_, pkk = psK.next()
                for hh in range(4):
                    TR(P, pk_[:, hh, :], kbf[:, n, hh * 128:(hh + 1) * 128], k.ident_b[:], ["kbf", "ident_b"], [pkk])
                CP(P, "act" if n % 2 else "dve", KTs[:, n, :, :], pk_, [pkk], ["KTs"])
            for h in range(NH):
                for n in range(NBK):
                    MM(P, psS[:, n * 32:(n + 1) * 32], KTs[:, n, h, :], Qblk[:, h, :], ["KTs", "Qblk"], ["psS"], start=True, stop=(n != NBK - 1))
                for c in range(2):
                    MM(P, psS[:, (NBK - 1) * 32 + c * 16:(NBK - 1) * 32 + (c + 1) * 16], k.ident_b[:], k.pat[:, h, 1, 0:16], ["ident_b", "pat"], ["psS"],
                       start=False, stop=(c == 1))
                MM(P, psN[:, :], KTn[:, h, :], Qblk[:, h, :], ["KTn", "Qblk"], ["psN"], start=True, stop=False)
                for c in range(2):
                    MM(P, psN[:, c * 16:(c + 1) * 16], k.ident_b[0:16, 0:16], k.pat[0:16, h, 0, 0:16], ["ident_b", "pat"], ["psN"],
                       start=False, stop=(c == 1))
                ACT(P, pTs[:], psS[:], AF.Exp, ["psS"], ["pTs"])
                ACT(P, pTn[:], psN[:], AF.Exp, ["psN"], ["pTn"])
                for c in range(2):
                    for n in range(NBK):
                        MM(P, psO[c][:, :], pTs[:, n * 32 + c * 16:n * 32 + (c + 1) * 16], Vs[:, n, h, :], ["pTs", "Vs"], ["psOs%d" % c],
                           start=(n == 0), stop=False)
                    MM(P, psO[c][:, :], pTn[:, c * 16:(c + 1) * 16], Vn[:, h, :], ["pTn", "Vn"], ["psOs%d" % c], start=False, stop=True)
                attn_post(P, k, h, 16, [(psO[0][:, :], "psOs0")], [(psO[1][:, :], "psOs1")],
                          lambda qs, h=h: sgs[:, h * 128:(h + 1) * 128], "sgs", None, wk, pst, "oats")
                P.dma("sp", k.d_oaT[h * 128:(h + 1) * 128, tk0:tk0 + 16], oat[:], reads=["oats"], writes=["oats"], tag="oats")
        P.emit()


class Rec:
    def __init__(self):
        self.items = []

    def op(self, eng, fn, reads=(), writes=()):
        self.items.append((eng, fn, reads, writes))


def phase3(k):
    nc = k.nc
    P = Prog(k.ctx)
    with ExitStack() as es:
        sb = lambda n, s, d=F32: es.enter_context(nc.sbuf_tensor(n, s, d))
        ps = lambda n, s, d=F32: es.enter_context(nc.psum_tensor(n, s, d))
        cf = k.cf
        NB = 2
        qT = [sb("g_qT%d" % i, [128, 4, 128], BF16) for i in range(NB)]
        kTt = [sb("g_kT%d" % i, [128, 4, 128], BF16) for i in range(NB)]
        kb = [sb("g_kb%d" % i, [128, 512], BF16) for i in range(NB)]
        vb = [sb("g_vb%d" % i, [128, 512], BF16) for i in range(NB)]
        szb = [sb("g_sz%d" % i, [128, 512], BF16) for i in range(NB)]
        gbt = [sb("g_gb%d" % i, [128, 8]) for i in range(NB)]
        st = [sb("g_s%d" % h, [128, 128]) for h in range(NH)]
        stb = [sb("g_sb%d" % h, [128, 128], BF16) for h in range(NH)]
        Gcol = sb("Gcol", [128, 4])
        nGcol = sb("nGcol", [128, 4])
        obt = sb("obt", [128, 512], BF16)
        obT = [sb("obT%d" % i, [128, 4, 128], BF16) for i in range(2)]
        f32n = ("gones", "Grow", "tmpa", "tmpb", "Dm", "DTm", "eGrow", "Lf", "Tnf", "TTf", "usb", "junk", "ob")
        bfn = ("Lo", "W1T", "Tnb", "Pk0", "Pk1", "PkT0", "PkT1", "TTb", "vbeta", "kbg", "kd", "qgT", "qkT", "wT", "vnew")
        coln = ("gl", "kds", "eGc", "bge", "ss")
        H = []
        for h in range(NH):
            d = {}
            for n_ in f32n:
                d[n_] = sb("h%d_%s" % (h, n_), [128, 128])
            for n_ in bfn:
                d[n_] = sb("h%d_%s" % (h, n_), [128, 128], BF16)
            for n_ in coln:
                d[n_] = sb("h%d_%s" % (h, n_), [128, 1])
            H.append(d)
        HB = [ps("gHB%d" % h, [128, 512]) for h in range(NH)]
        SB = ps("gSB", [128, 8, 128], BF16)
        GC = ps("gGC", [128, 512])
        p_ot = SB[:, 0:4, :]
        p_gc = GC[:, 0:4]
        P.bank["p_ot"] = "gSB"
        P.bank["p_gc"] = "gGC"
        PS = []
        for h in range(NH):
            A_, B_, C_ = HB[h][:, 0:128], HB[h][:, 128:256], HB[h][:, 256:384]
            d = dict(p_g=A_, p_a=A_, p_u=A_, p_ws=A_, p_kk=B_, p_b=B_, p_w=B_, p_o=B_, p_c=C_, p_qk=C_, p_s=C_, p_tr=SB[:, 4 + h, :])
            PS.append(d)
            for n_, sl_ in (("p_g", "A"), ("p_a", "A"), ("p_u", "A"), ("p_ws", "A"), ("p_kk", "B"), ("p_b", "B"), ("p_w", "B"), ("p_o", "B"),
                            ("p_c", "C"), ("p_qk", "C"), ("p_s", "C")):
                P.bank["pslot%s_%d" % (sl_, h)] = "gHB%d" % h
            P.bank["pslotT_%d" % h] = "gSB"
        SLOT = dict(p_g="A", p_a="A", p_u="A", p_ws="A", p_kk="B", p_b="B", p_w="B", p_o="B", p_c="C", p_qk="C", p_s="C", p_tr="T")
        ident = k.ident_b

        def head_ops(R, C, bi, h, nsteps):
            kq, kk_, kkb, kvb, ksz, kgb = ["g_%s%d" % (n, bi) for n in ("qT", "kT", "kb", "vb", "sz", "gb")]
            g8 = gbt[bi]
            T_ = H[h]
            pp = PS[h]
            K_ = lambda n: "h%d_%s" % (h, n)
            pk_ = lambda n: "pslot%s_%d" % (SLOT[n], h)
            hs = slice(h * 128, (h + 1) * 128)
            gones, Grow, tmpa, tmpb, Dm, DTm, eGrow, Lf, Tnf, TTf, usb, junk, ob = [T_[n] for n in f32n]
            Lo, W1T, Tnb, Pk0, Pk1, PkT0, PkT1, TTb, vbeta, kbg, kd, qgT, qkT, wT, vnew = [T_[n] for n in bfn]
            gl, kds, eGc, bge, ss = [T_[n] for n in coln]
            Pk, PkT = [Pk0, Pk1], [PkT0, PkT1]
            p_g, p_kk, p_qk, p_u, p_a, p_b, p_c, p_w, p_ws, p_o, p_s, p_tr = [pp[n] for n in
                ("p_g", "p_kk", "p_qk", "p_u", "p_a", "p_b", "p_c", "p_w", "p_ws", "p_o", "p_s", "p_tr")]
            TS(R, "dve", gones[0:C, :], cf[0:C, C_ONE:C_ONE + 128], g8[0:C, h:h + 1], None, ALU.mult, None, ["cf", kgb], [K_("gones")])
            MM(R, p_g[:, 0:C], gones[0:C, :], cf[0:C, C_TRIU:C_TRIU + C], [K_("gones"), "cf"], [pk_("p_g")])
            CP(R, "act", Grow[:, 0:C], p_g[:, 0:C], [pk_("p_g")], [K_("Grow")])
            TTo(R, "dve", tmpa[0:C, 0:C], cf[0:C, C_MS:C_MS + C], Grow[0:C, 0:C], ALU.subtract, ["cf", K_("Grow")], [K_("tmpa")])
            ACT(R, Dm[0:C, 0:C], tmpa[0:C, 0:C], AF.Exp, [K_("tmpa"), "Gcol"], [K_("Dm")], bias=Gcol[0:C, h:h + 1], scale=1.0)
            TTo(R, "dve", tmpb[0:C, 0:C], cf[0:C, C_MDT:C_MDT + C], Grow[0:C, 0:C], ALU.add, ["cf", K_("Grow")], [K_("tmpb")])
            ACT(R, DTm[0:C, 0:C], tmpb[0:C, 0:C], AF.Exp, [K_("tmpb"), "nGcol"], [K_("DTm")], bias=nGcol[0:C, h:h + 1], scale=1.0)
            ACT(R, eGrow[:, 0:C], Grow[:, 0:C], AF.Exp, [K_("Grow")], [K_("eGrow")])
            ACT(R, gl[:, :], Grow[:, C - 1:C], AF.Exp, [K_("Grow")], [K_("gl")])
            ACT(R, kds[0:C, :], Gcol[0:C, h:h + 1], AF.Exp, ["Gcol", K_("Grow")], [K_("kds")], bias=Grow[0:C, C - 1:C], scale=-1.0)
            ACT(R, eGc[0:C, :], Gcol[0:C, h:h + 1], AF.Exp, ["Gcol"], [K_("eGc")])
            TTo(R, "dve", bge[0:C, :], g8[0:C, 4 + h:5 + h], eGc[0:C, :], ALU.mult, [kgb, K_("eGc")], [K_("bge")])
            MM(R, p_kk[0:C, 0:C], kTt[bi][:, h, 0:C], kTt[bi][:, h, 0:C], [kk_], [pk_("p_kk")])
            STT(R, "dve", Lf[0:C, 0:C], p_kk[0:C, 0:C], g8[0:C, 4 + h:5 + h], Dm[0:C, 0:C], ALU.mult, ALU.mult,
                [pk_("p_kk"), kgb, K_("Dm")], [K_("Lf")])
            if C == 128:
                TTo(R, "dve", Pk[0][:, :], Lf[:, :], cf[:, C_M16:C_M16 + 128], ALU.mult, [K_("Lf"), "cf"], [K_("Pk0")])
            else:
                TS(R, "dve", Pk[0][0:C, 0:C], Lf[0:C, 0:C], -1.0, None, ALU.mult, None, [K_("Lf")], [K_("Pk0")])
            TR(R, p_tr[0:C, 0:C], Pk[0][0:C, 0:C], ident[0:C, 0:C], [K_("Pk0"), "ident_b"], [pk_("p_tr")])
            CP(R, "act", PkT[0][0:C, 0:C], p_tr[0:C, 0:C], [pk_("p_tr")], [K_("PkT0")])
            TTo(R, "dve", TTf[0:C, 0:C], p_tr[0:C, 0:C], cf[0:C, C_ID:C_ID + C], ALU.add, [pk_("p_tr"), "cf"], [K_("TTf")])
            CP(R, "act", TTb[0:C, 0:C], TTf[0:C, 0:C], [K_("TTf")], [K_("TTb")])
            cur = 0
            for stp in range(nsteps):
                nx = 1 - cur
                MM(R, p_a[0:C, 0:C], PkT[cur][0:C, 0:C], Pk[cur][0:C, 0:C], [K_("Pk%d" % cur), K_("PkT%d" % cur)], [pk_("p_a")])
                CP(R, "act", Pk[nx][0:C, 0:C], p_a[0:C, 0:C], [pk_("p_a")], [K_("Pk%d" % nx)])
                if stp != nsteps - 1:
                    MM(R, p_b[0:C, 0:C], Pk[cur][0:C, 0:C], PkT[cur][0:C, 0:C], [K_("Pk%d" % cur), K_("PkT%d" % cur)], [pk_("p_b")])
                    CP(R, "dve", PkT[nx][0:C, 0:C], p_b[0:C, 0:C], [pk_("p_b")], [K_("PkT%d" % nx)])
                MM(R, p_c[0:C, 0:C], Pk[nx][0:C, 0:C], TTb[0:C, 0:C], [K_("Pk%d" % nx), K_("TTb")], [pk_("p_c")])
                TTo(R, "dve", TTf[0:C, 0:C], p_c[0:C, 0:C], TTf[0:C, 0:C], ALU.add, [pk_("p_c"), K_("TTf")], [K_("TTf")])
                CP(R, "act", TTb[0:C, 0:C], TTf[0:C, 0:C], [K_("TTf")], [K_("TTb")])
                cur = nx
            if C == 128:
                TR(R, p_tr[:, :], TTb[:, :], ident[:, :], [K_("TTb"), "ident_b"], [pk_("p_tr")])
                CP(R, "act", Tnf[:, :], p_tr[:, :], [pk_("p_tr")], [K_("Tnf")])
                CP(R, "dve", Tnb[:, :], p_tr[:, :], [pk_("p_tr")], [K_("Tnb")])
                for lv in range(3):
                    mc = C_MLO + lv * 128
                    TTo(R, "dve", Lo[:, :], Lf[:, :], cf[:, mc:mc + 128], ALU.mult, [K_("Lf"), "cf"], [K_("Lo")])
                    MM(R, p_a[:, :], Lo[:, :], TTb[:, :], [K_("Lo"), K_("TTb")], [pk_("p_a")])
                    CP(R, "act", W1T[:, :], p_a[:, :], [pk_("p_a")], [K_("W1T")])
                    MM(R, p_c[:, :], Tnb[:, :], W1T[:, :], [K_("Tnb"), K_("W1T")], [pk_("p_c")])
                    if lv < 2:
                        MM(R, p_b[:, :], W1T[:, :], Tnb[:, :], [K_("W1T"), K_("Tnb")], [pk_("p_b")])
                    TTo(R, "dve", TTf[:, :], TTf[:, :], p_c[:, :], ALU.subtract, [K_("TTf"), pk_("p_c")], [K_("TTf")])
                    CP(R, "act", TTb[:, :], TTf[:, :], [K_("TTf")], [K_("TTb")])
                    if lv < 2:
                        TTo(R, "dve", Tnf[:, :], Tnf[:, :], p_b[:, :], ALU.subtract, [K_("Tnf"), pk_("p_b")], [K_("Tnf")])
                        CP(R, "act", Tnb[:, :], Tnf[:, :], [K_("Tnf")], [K_("Tnb")])
            TS(R, "dve", vbeta[0:C, :], vb[bi][0:C, hs], g8[0:C, 4 + h:5 + h], None, ALU.mult, None, [kvb, kgb], [K_("vbeta")])
            TS(R, "dve", kbg[0:C, :], kb[bi][0:C, hs], bge[0:C, :], None, ALU.mult, None, [kkb, K_("bge")], [K_("kbg")])
            TS(R, "dve", kd[0:C, :], kb[bi][0:C, hs], kds[0:C, :], None, ALU.mult, None, [kkb, K_("kds")], [K_("kd")])
            MM(R, p_u[0:C, :], TTb[0:C, 0:C], vbeta[0:C, :], [K_("TTb"), K_("vbeta")], [pk_("p_u")])
            CP(R, "act", usb[0:C, :], p_u[0:C, :], [pk_("p_u")], [K_("usb")])
            MM(R, p_w[:, 0:C], kbg[0:C, :], TTb[0:C, 0:C], [K_("kbg"), K_("TTb")], [pk_("p_w")])
            CP(R, "act", wT[:, 0:C], p_w[:, 0:C], [pk_("p_w")], [K_("wT")])
            MM(R, p_qk[0:C, 0:C], kTt[bi][:, h, 0:C], qT[bi][:, h, 0:C], [kk_, kq], [pk_("p_qk")])
            TTo(R, "dve", qkT[0:C, 0:C], p_qk[0:C, 0:C], DTm[0:C, 0:C], ALU.mult, [pk_("p_qk"), K_("DTm")], [K_("qkT")])
            TTo(R, "dve", qgT[:, 0:C], qT[bi][:, h, 0:C], eGrow[:, 0:C], ALU.mult, [kq, K_("eGrow")], [K_("qgT")])
            sk, sbk = "g_s%d" % h, "g_sb%d" % h
            MM(R, p_ws[0:C, :], wT[:, 0:C], stb[h][:], [K_("wT"), sbk], [pk_("p_ws")])
            TTo(R, "dve", vnew[0:C, :], usb[0:C, :], p_ws[0:C, :], ALU.subtract, [K_("usb"), pk_("p_ws")], [K_("vnew")])
            MM(R, p_o[0:C, :], qgT[:, 0:C], stb[h][:], [K_("qgT"), sbk], [pk_("p_o")], start=True, stop=False)
            MM(R, p_o[0:C, :], qkT[0:C, 0:C], vnew[0:C, :], [K_("qkT"), K_("vnew")], [pk_("p_o")], start=False, stop=True)
            MM(R, p_s[:, :], kd[0:C, :], vnew[0:C, :], [K_("kd"), K_("vnew")], [pk_("p_s")])
            STT(R, "dve", st[h][:], st[h][:], gl[:, :], p_s[:, :], ALU.mult, ALU.add, [sk, K_("gl"), pk_("p_s")], [sk])
            CP(R, "act", stb[h][:], st[h][:], [sk], [sbk])
            ACT(R, junk[0:C, :], p_o[0:C, :], AF.Square, [pk_("p_o")], [K_("junk"), K_("ss")], accum_out=ss[0:C, :])
            ACT(R, ss[0:C, :], ss[0:C, :], AF.Sqrt, [K_("ss"), "eps6"], [K_("ss")], bias=k.eps6[0:C, :], scale=1.0 / 128.0)
            RCP(R, ss[0:C, :], ss[0:C, :], [K_("ss")], [K_("ss")])
            STT(R, "dve", ob[0:C, :], p_o[0:C, :], ss[0:C, :], k.dnw[0:C, :], ALU.mult, ALU.mult, [pk_("p_o"), K_("ss"), "dnw"], [K_("ob")])
            TTo(R, "dve", obt[0:C, hs], ob[0:C, :], szb[bi][0:C, hs], ALU.mult, [K_("ob"), ksz], ["obt%d" % h])

        cnt = [0]

        def chunk(C, tok0, bi, nsteps):
            kgb = "g_gb%d" % bi
            g8 = gbt[bi]
            MM(P, p_gc[0:C, :], cf[0:C, C_TRIU:C_TRIU + C], g8[0:C, 0:4], ["cf", kgb], ["p_gc"])
            CP(P, "dve", Gcol[0:C, :], p_gc[0:C, :], ["p_gc"], ["Gcol"])
            TS(P, "dve", nGcol[0:C, :], p_gc[0:C, :], -1.0, None, ALU.mult, None, ["p_gc"], ["nGcol"])
            recs = []
            for h in range(NH):
                R = Rec()
                head_ops(R, C, bi, h, nsteps)
                recs.append(R.items)
            for i in range(max(len(r) for r in recs)):
                for h in range(NH):
                    if i < len(recs[h]):
                        P.op(*recs[h][i])
            oi = cnt[0] % 2
            cnt[0] += 1
            for h in range(NH):
                TR(P, p_ot[:, h, 0:C], obt[0:C, h * 128:(h + 1) * 128], ident[0:C, 0:C], ["obt%d" % h, "ident_b"], ["p_ot"])
            CP(P, "act", obT[oi][:, :, 0:C], p_ot[:, :, 0:C], ["p_ot"], ["obT%d" % oi])
            P.dma("sp", k.d_obT[:, tok0:tok0 + C].rearrange("(h p) t -> p h t", p=128), obT[oi][:, :, 0:C], reads=["obT%d" % oi],
                  writes=["obT%d" % oi], tag="obT%d" % oi)

        def load(C, tok0, bi):
            kq, kk_, kkb, kvb, ksz, kgb = ["g_%s%d" % (n, bi) for n in ("qT", "kT", "kb", "vb", "sz", "gb")]
            P.dma("sp", qT[bi][:, :, 0:C], k.d_qbT[:, tok0:tok0 + C].rearrange("(h p) t -> p h t", p=128), writes=[kq], tag=kq)
            P.dma("pool", kTt[bi][:, :, 0:C], k.d_kbT[:, tok0:tok0 + C].rearrange("(h p) t -> p h t", p=128), writes=[kk_], tag=kk_)
            P.dma("sp", kb[bi][0:C, :], k.d_kb[tok0:tok0 + C, :], writes=[kkb], tag=kkb)
            P.dma("pool", vb[bi][0:C, :], k.d_vbn[tok0:tok0 + C, :], writes=[kvb], tag=kvb)
            P.dma("sp", szb[bi][0:C, :], k.d_sz[tok0:tok0 + C, :], writes=[ksz], tag=ksz)
            P.dma("pool", gbt[bi][0:C, :], k.d_gbeta[tok0:tok0 + C, :], writes=[kgb], tag=kgb)

        for h in range(NH):
            P.op("pool", lambda e, h=h: e.memset(st[h][:], 0.0), (), ["g_s%d" % h])
            P.op("pool", lambda e, h=h: e.memset(stb[h][:], 0.0), (), ["g_sb%d" % h])
        nch = S // 128
        load(128, 0, 0)
        for n in range(nch):
            if n + 1 < nch:
                load(128, (n + 1) * 128, (n + 1) % 2)
            chunk(128, n * 128, n % 2, 3)
        for h in range(NH):
            P.dma("sp", k.o_ndp[h], st[h][:], reads=["g_s%d" % h], writes=["g_s%d" % h], tag="ndp%d" % h)
        for s in range(4):
            bi = s % 2
            for h in range(NH):
                P.dma("pool", st[h][:], k.d_sdelta[s, h], writes=["g_s%d" % h], tag="ndp%d" % h)
                CP(P, "dve", stb[h][:], st[h][:], ["g_s%d" % h], ["g_sb%d" % h])
            load(16, S + s * 16, bi)
            chunk(16, S + s * 16, bi, 3)
            for h in range(NH):
                P.dma("sp", k.o_nds[s, h], st[h][:], reads=["g_s%d" % h], writes=["g_s%d" % h], tag="ndp%d" % h)
        P.emit()


def phase4(k):
    nc = k.nc
    alpha = (2.0 * 1) ** 0.25
    with ExitStack() as es:
        sb = lambda n, s, d=F32: es.enter_context(nc.sbuf_tensor(n, s, d))
        ps = lambda n, s, d=F32: es.enter_context(nc.psum_tensor(n, s, d))
        wpa = sb("wpa", [128, 4, 1024], BF16)
        wpb = sb("wpb", [128, 4, 1024], BF16)
        wo = sb("wo", [128, 8, 1024], BF16)
        wst = [sb("w4st%d" % i, [128, 4, 1024]) for i in range(2)]
        P = Prog(k.ctx)
        srcs = [(k.d_wpa, wpa, 0, 4), (k.d_wpb, wpb, 0, 4), (k.d_wout, wo, 0, 4), (k.d_wout, wo, 4, 4)]
        for i, (src, dst, k0, nk) in enumerate(srcs):
            w = wst[i % 2]
            wk = "w4st%d" % (i % 2)
            P.dma(_dq(i), w[:], src.rearrange("(kc p) n -> p kc n", p=128)[:, k0:k0 + nk, :], writes=[wk], tag=wk)
            for kc in range(nk):
                CP(P, "dve" if kc % 2 else "act", dst[:, k0 + kc, :], w[:, kc, :], [wk], ["w4"])
        oa = [sb("oa%d" % i, [128, 4, 512], BF16) for i in range(2)]
        obb = [sb("ob%d" % i, [128, 4, 512], BF16) for i in range(2)]
        ma = [sb("ma%d" % i, [128, 8, 512], BF16) for i in range(2)]
        mb = [sb("mb%d" % i, [128, 8, 512], BF16) for i in range(2)]
        mT = sb("mT", [128, 8, 512], BF16)
        t1 = sb("t1", [128, 512])
        t2 = sb("t2", [128, 512])
        xr = [sb("xr%d" % i, [128, 1024]) for i in range(2)]
        z = sb("z", [128, 1024])
        st = sb("st4", [128, 2, 6])
        mv = sb("mv4", [128, 2])
        rstd = sb("rstd4", [128, 1])
        yo = [sb("yo%d" % i, [128, 1024]) for i in range(2)]
        psA = ps("p4a", [128, 512])
        psB = ps("p4b", [128, 512])
        psY = [ps("p4y%d" % i, [128, 512]) for i in range(2)]
        for n_ in ("p4a", "p4b", "p4y0", "p4y1"):
            P.bank[n_] = n_
        tiles = [(i * 512, 4, 128, False) for i in range(S // 512)] + [(S, 1, 64, True)]
        for ti, (t0, nsub, R, smp) in enumerate(tiles):
            T = nsub * R
            b = ti % 2
            ks = ["oa%d" % b, "ob%d" % b, "ma%d" % b, "mb%d" % b]
            P.dma("sp", oa[b][:, :, 0:T], k.d_oaT[:, t0:t0 + T].rearrange("(c p) t -> p c t", p=128), writes=[ks[0]], tag=ks[0])
            P.dma("pool", obb[b][:, :, 0:T], k.d_obT[:, t0:t0 + T].rearrange("(c p) t -> p c t", p=128), writes=[ks[1]], tag=ks[1])
            P.dma("sp", ma[b][:, :, 0:T], k.d_smA[:, t0:t0 + T].rearrange("(c p) t -> p c t", p=128), writes=[ks[2]], tag=ks[2])
            P.dma("pool", mb[b][:, :, 0:T], k.d_smB[:, t0:t0 + T].rearrange("(c p) t -> p c t", p=128), writes=[ks[3]], tag=ks[3])
            for fc in range(8):
                fs = slice(fc * 128, (fc + 1) * 128)
                for kc in range(4):
                    MM(P, psA[:, 0:T], wpa[:, kc, fs], oa[b][:, kc, 0:T], ["w4", ks[0]], ["p4a"], start=(kc == 0), stop=(kc == 3))
                for kc in range(4):
                    MM(P, psB[:, 0:T], wpb[:, kc, fs], obb[b][:, kc, 0:T], ["w4", ks[1]], ["p4b"], start=(kc == 0), stop=(kc == 3))
                TTo(P, "dve", t1[:, 0:T], psA[:, 0:T], ma[b][:, fc, 0:T], ALU.mult, ["p4a", ks[2]], ["t1"])
                TTo(P, "dve", t2[:, 0:T], psB[:, 0:T], mb[b][:, fc, 0:T], ALU.mult, ["p4b", ks[3]], ["t2"])
                TTo(P, "dve", mT[:, fc, 0:T], t1[:, 0:T], t2[:, 0:T], ALU.add, ["t1", "t2"], ["mT"])
            for j in range(nsub):
                tok0 = t0 + j * R
                xs, xk = xr[j % 2], "xr%d" % (j % 2)
                P.dma(_dq(j), xs[0:R, :], k.d_xs if smp else k.d_xp[tok0:tok0 + R, :], writes=[xk], tag=xk)
                gate = k.gate_s if smp else k.gate_p
                for hf in range(2):
                    for fc in range(8):
                        MM(P, psY[hf][0:R, :], mT[:, fc, j * R:(j + 1) * R], wo[:, fc, hf * 512:(hf + 1) * 512], ["mT", "w4"], ["p4y%d" % hf],
                           start=(fc == 0), stop=(fc == 7))
                    TTo(P, "dve", z[0:R, hf * 512:(hf + 1) * 512], psY[hf][0:R, :], gate[0:R, hf * 512:(hf + 1) * 512], ALU.mult,
                        ["p4y%d" % hf, "gate_p", "gate_s"], ["z"])
                STT(P, "dve", z[0:R, :], xs[0:R, :], alpha, z[0:R, :], ALU.mult, ALU.add, [xk, "z"], ["z"])
                P.op("dve", lambda e, a=st[0:R, 0, :], b_=z[0:R, 0:512]: e.bn_stats(a, b_), ["z"], ["st40"])
                P.op("dve", lambda e, a=st[0:R, 1, :], b_=z[0:R, 512:1024]: e.bn_stats(a, b_), ["z"], ["st41"])
                P.op("dve", lambda e, a=mv[0:R, :], b_=st[0:R, :, :]: e.bn_aggr(a, b_), ["st40", "st41"], ["mv4"])
                ACT(P, rstd[0:R, :], mv[0:R, 1:2], AF.Sqrt, ["mv4", "eps5"], ["rstd4"], bias=k.eps5[0:R, :], scale=1.0)
                RCP(P, rstd[0:R, :], rstd[0:R, :], ["rstd4"], ["rstd4"])
                TS(P, "dve", z[0:R, :], z[0:R, :], mv[0:R, 0:1], rstd[0:R, :], ALU.subtract, ALU.mult, ["z", "mv4", "rstd4"], ["z"])
                y, yk = yo[j % 2], "yo%d" % (j % 2)
                TTo(P, "dve", y[0:R, :], z[0:R, :], k.lng[0:R, :], ALU.mult, ["z", "lng"], [yk])
                TTo(P, "dve", y[0:R, :], y[0:R, :], k.lnb[0:R, :], ALU.add, [yk, "lnb"], [yk])
                P.dma(_dq(j + 1), k.o_ys[:, :] if smp else k.o_yp[tok0:tok0 + R, :], y[0:R, :], reads=[yk], writes=[yk], tag=yk)
        P.emit()


def build_program():
    nc = bass.Bass("TRN2", target_bir_lowering=False)
    k = K()
    k.nc = nc
    k.ctx = Ctx(nc)
    din = lambda n, s: nc.dram_tensor(n, s, F32, kind="ExternalInput").ap()
    dout = lambda n, s: nc.dram_tensor(n, s, F32, kind="ExternalOutput").ap()
    k.d_xp = din("x_p", [S, D])
    k.d_xs = din("x_s", [NSM, D])
    k.d_c5T = din("c5T", [128, 8, 5])
    k.d_ck = din("ck", [4, PAST, 512])
    k.d_cv = din("cv", [4, PAST, 512])
    k.d_sconv = din("sconv", [4, 3, 1536])
    k.d_sdelta = din("sdelta", [4, 4, 128, 128])
    k.d_wada = din("w_ada", [D, 3 * D])
    k.d_bada = din("b_ada", [3 * D])
    k.d_badaT = din("b_adaT", [128, 24])
    k.d_win = din("w_in", [D, WIN])
    k.d_small = {}
    for n, s in (("lam_q1", 64), ("lam_k1", 64), ("lam_q2", 64), ("lam_k2", 64), ("subln_w", 128), ("a_log", 4), ("dt_bias", 4),
                 ("dn_norm_w", 128), ("ln_g", D), ("ln_b", D)):
        k.d_small[n] = din(n, [s])
    k.d_convwT = din("conv_wT", [128, 12, 4])
    k.d_wpa = din("w_pa", [512, D])
    k.d_wpb = din("w_pb", [512, D])
    k.d_wout = din("w_out", [D, D])
    k.d_rel = din("rel_table", [32, 4])
    k.d_consts = din("consts", [128, C_END])
    k.o_yp = dout("y_p", [S, D])
    k.o_ys = dout("y_s", [NSM, D])
    k.o_nkp = dout("nk_p", [S, 512])
    k.o_nvp = dout("nv_p", [S, 512])
    k.o_ncp = dout("nc_p", [3, 1536])
    k.o_ndp = dout("nd_p", [4, 128, 128])
    k.o_nks = dout("nk_s", [NSM, 512])
    k.o_nvs = dout("nv_s", [NSM, 512])
    k.o_ncs = dout("nc_s", [4, 3, 1536])
    k.o_nds = dout("nd_s", [4, 4, 128, 128])
    scr = lambda n, s, d=BF16: nc.dram_tensor(n, s, d, kind="ExternalOutput" if DEBUG_SCR else "Internal").ap()
    k.d_qT = scr("s_qT", [512, TT])
    k.d_kT = scr("s_kT", [512, TT])
    k.d_vb = scr("s_vb", [TT, 512])
    k.d_sg = scr("s_sg", [TT, 512])
    k.d_sgT = scr("s_sgT", [512, TT])
    k.d_qbT = scr("s_qbT", [512, TT])
    k.d_kbT = scr("s_kbT", [512, TT])
    k.d_kb = scr("s_kb", [TT, 512])
    k.d_vbn = scr("s_vbn", [TT, 512])
    k.d_sz = scr("s_sz", [TT, 512])
    k.d_gbeta = scr("s_gbeta", [TT, 8], F32)
    k.d_smA = scr("s_smA", [D, TT])
    k.d_smB = scr("s_smB", [D, TT])
    k.d_oaT = scr("s_oaT", [512, TT])
    k.d_obT = scr("s_obT", [512, TT])
    k.d_R_t = nc.dram_tensor("s_R", [4, 128, 384], F32, kind="Internal")
    k.d_R = k.d_R_t.ap()
    pb = lambda n, s, d=F32: nc.alloc_sbuf_tensor(n, s, d)
    k.cf = pb("cf", [128, C_END])
    k.ident_b = pb("ident_b", [128, 128], BF16)
    k.ones_b = pb("ones_b", [128, 128], BF16)
    k.modT = pb("modT", [128, 24, 5])
    k.gate_p = pb("gate_p", [128, D])
    k.gate_s = pb("gate_s", [64, D])
    k.lng = pb("lng", [128, D])
    k.lnb = pb("lnb", [128, D])
    k.nlam = pb("nlam", [128, 1])
    k.wsub = pb("wsub", [128, 128])
    k.wsubc = pb("wsubc", [128, 1])
    k.dnw = pb("dnw", [128, 128])
    k.convw = pb("convw", [128, 12, 4])
    k.dtb = pb("dtb", [128, 4])
    k.negA = pb("negA", [128, 4])
    k.eps5 = pb("eps5", [128, 1])
    k.eps6 = pb("eps6", [128, 1])
    k.pat = pb("pat", [128, 4, 2, 128], BF16)
    phs = [phase0, phase1, phase2, phase2s, phase3, phase4]
    for i, f in enumerate(phs):
        if i < PHASES and i not in SKIP:
            f(k)
    return nc


def _rel_bucket_np(rel):
    nb, max_exact = 16, 8
    n = np.abs(rel)
    lg = np.log(np.maximum(n, 1).astype(np.float32) / np.float32(max_exact)) / np.float32(math.log(128 / max_exact)) * np.float32(nb - max_exact)
    large = max_exact + lg.astype(np.float32).astype(np.int32)
    large = np.minimum(large, nb - 1)
    return np.where(rel > 0, nb, 0) + np.where(n < max_exact, n, large)


def _consts():
    c = np.zeros((128, C_END), np.float32)
    i = np.arange(128)[:, None]
    j = np.arange(128)[None, :]
    c[:, C_ID:C_ID + 128] = (i == j)
    c[:, C_TRIU:C_TRIU + 128] = (i <= j)
    c[:, C_MS:C_MS + 128] = np.where(i > j, 0.0, NEG)
    c[:, C_MDT:C_MDT + 128] = np.where(j >= i, 0.0, NEG)
    c[:, C_CM:C_CM + 128] = np.where((i // 64) <= (j // 64), 0.0, NEG)
    c[:, C_ONE:C_ONE + 128] = 1.0
    c[0, C_SEL:C_SEL + 128] = 1.0
    for p in range(64):
        c[1 + p // 16, C_SEL + 128 + p] = 1.0
    c[:, C_M16:C_M16 + 128] = np.where((i // 16) == (j // 16), -1.0, 0.0)
    for lv, bsz in enumerate((16, 32, 64)):
        c[:, C_MLO + lv * 128:C_MLO + (lv + 1) * 128] = ((i // (2 * bsz)) == (j // (2 * bsz))) & ((i // bsz) % 2 == 1) & ((j // bsz) % 2 == 0)
    rel = 127 - np.arange(384)
    bk = _rel_bucket_np(rel)
    for jj in range(384):
        c[bk[jj], C_OH + jj] += 1.0
        c[15, C_OH + jj] -= 1.0
    return c


_CACHE = {}


def kernel(**inp):
    f = lambda a: np.ascontiguousarray(np.asarray(a, dtype=np.float32))
    if "nc" not in _CACHE:
        _CACHE["nc"] = build_program()
    nc = _CACHE["nc"]
    consts = _consts()
    shared = {
        "w_ada": f(inp["w_ada"][0]), "b_ada": f(inp["b_ada"][0]),
        "b_adaT": f(np.asarray(inp["b_ada"][0]).reshape(24, 128).T),
        "w_in": f(inp["w_in"][0]),
        "conv_wT": f(np.asarray(inp["conv_w"][0]).reshape(4, 12, 128).transpose(2, 1, 0)),
        "w_pa": f(inp["w_pa"][0]), "w_pb": f(inp["w_pb"][0]), "w_out": f(inp["w_out"][0]),
        "rel_table": f(inp["rel_table"]), "consts": consts,
    }
    for n in ("lam_q1", "lam_k1", "lam_q2", "lam_k2", "subln_w", "a_log", "dt_bias", "dn_norm_w", "ln_g", "ln_b"):
        shared[n] = f(inp[n][0])
    xp, xsm = np.asarray(inp["x_prompt"]), np.asarray(inp["x_sample"])
    cp, cs = np.asarray(inp["c_prompt"]), np.asarray(inp["c_sample"])
    ck, cv = np.asarray(inp["cache_k"][0]), np.asarray(inp["cache_v"][0])
    sc, sd = np.asarray(inp["state_conv"][0]), np.asarray(inp["state_delta"][0])
    in_maps = []
    for c in range(8):
        sl = slice(4 * c, 4 * c + 4)
        c5 = np.concatenate([cp[c:c + 1], cs[sl]], axis=0)
        m = dict(shared)
        m["x_p"] = f(xp[c])
        m["x_s"] = f(xsm[sl].reshape(NSM, D))
        m["c5T"] = f(c5.reshape(5, 8, 128).transpose(2, 1, 0))
        m["ck"] = f(ck[sl].reshape(4, PAST, 512))
        m["cv"] = f(cv[sl].reshape(4, PAST, 512))
        m["sconv"] = f(sc[sl])
        m["sdelta"] = f(sd[sl])
        in_maps.append(m)
    res = run_bass_kernel_spmd(nc, in_maps, core_ids=list(range(8)), **({'trace': True} if TRACE else {}))
    if TRACE:
        print('EXEC_NS', res.exec_time_ns, flush=True)
    r = res.results
    global LAST
    LAST = r
    cat = lambda n: np.stack([np.asarray(r[c][n]) for c in range(8)], axis=0)
    y_p = cat("y_p")
    y_s = cat("y_s").reshape(32, 16, D)
    nk_p = cat("nk_p").reshape(1, 8, S, 4, 128)
    nv_p = cat("nv_p").reshape(1, 8, S, 4, 128)
    nc_p = cat("nc_p").reshape(1, 8, 3, 1536)
    nd_p = cat("nd_p").reshape(1, 8, 4, 128, 128)
    nk_s = cat("nk_s").reshape(1, 32, 16, 4, 128)
    nv_s = cat("nv_s").reshape(1, 32, 16, 4, 128)
    nc_s = cat("nc_s").reshape(1, 32, 3, 1536)
    nd_s = cat("nd_s").reshape(1, 32, 4, 128, 128)
    return (y_p, y_s, nk_p, nv_p, nc_p, nd_p, nk_s, nv_s, nc_s, nd_s)
```
